# Optimizing a Trainium2 kernel written in Bass

```python
import jax, jax.numpy as jnp
from jax import lax
import numpy as np

D_MODEL = 1024
BATCH = 4
SEQ = 4096
DEPTH = 4
DEC_BATCH = 128
DEC_SEQ = 4
PAST_LEN = 2048
PAGE_SIZE = 128

N_MIXERS = 3
N_LRU_LAYERS = (DEPTH + 2) // 3
N_CONV_LAYERS = (DEPTH + 1) // 3
N_ATTN_LAYERS = DEPTH // 3

D_RNN = ((4 * D_MODEL // 3 + 127) // 128) * 128
LRU_BLOCKS = 16
LRU_BLOCK_DIM = D_RNN // LRU_BLOCKS
LRU_CONV_W = 4
LRU_C = 8.0

D_CONV = D_MODEL
CM_KERNEL = 31

ATT_GROUPS = ((128, 1), (512, 4), (2048, 16))
N_GROUPS = len(ATT_GROUPS)
ATT_HEADS = 8
HEAD_DIM = D_MODEL // ATT_HEADS
ATT_BLOCK = 128
ROPE_THETA = 10000.0

D_FF = 4 * D_MODEL
EPS = 1e-6

kernel_name = 'hybrid_rglru_conformer_dilated_swa_step'


def rms_norm(x, g):
    xf = x.astype(jnp.float32)
    y = xf * lax.rsqrt(jnp.mean(xf * xf, axis=-1, keepdims=True) + EPS)
    return (y * g.astype(jnp.float32)).astype(x.dtype)


def layer_norm(x, g, b):
    xf = x.astype(jnp.float32)
    mu = jnp.mean(xf, axis=-1, keepdims=True)
    var = jnp.mean(jnp.square(xf - mu), axis=-1, keepdims=True)
    return ((xf - mu) * lax.rsqrt(var + EPS) * g.astype(jnp.float32) + b.astype(jnp.float32)).astype(x.dtype)


def causal_dwconv(x, prev, w, b):
    x_ext = jnp.concatenate([prev.astype(x.dtype), x], axis=1)
    y = lax.conv_general_dilated(x_ext, w[:, None, :].astype(x.dtype), window_strides=(1,), padding='VALID',
                                 dimension_numbers=('NWC', 'WIO', 'NWC'), feature_group_count=x.shape[-1])
    return y + b, x_ext[:, x_ext.shape[1] - (w.shape[0] - 1):]


def _lru_combine(c1, c2):
    a1, b1 = c1
    a2, b2 = c2
    return a1 * a2, a2 * b1 + b2


def rglru_mixer(x, conv_prev, h_prev, w_in, conv_w, conv_b, ga_w, ga_b, gx_w, gx_b, lam, w_out):
    B, T, _ = x.shape
    gate, u = jnp.split(x @ w_in, 2, axis=-1)
    u, conv_state = causal_dwconv(u, conv_prev, conv_w, conv_b)
    ub = u.reshape(B, T, LRU_BLOCKS, LRU_BLOCK_DIM)
    r = jax.nn.sigmoid(jnp.einsum('btnc,ncd->btnd', ub, ga_w).reshape(B, T, D_RNN) + ga_b)
    i = jax.nn.sigmoid(jnp.einsum('btnc,ncd->btnd', ub, gx_w).reshape(B, T, D_RNN) + gx_b)
    log_a = -LRU_C * r.astype(jnp.float32) * jax.nn.softplus(-lam.astype(jnp.float32))
    a = jnp.exp(log_a)
    bterm = jnp.sqrt(-jnp.expm1(2.0 * log_a)) * (i * u).astype(jnp.float32)
    bterm = bterm.at[:, 0].add(a[:, 0] * h_prev.astype(jnp.float32))
    _, h = lax.associative_scan(_lru_combine, (a, bterm), axis=1)
    y = (h.astype(x.dtype) * jax.nn.gelu(gate)) @ w_out
    return y, conv_state, h[:, -1].astype(h_prev.dtype)


def conformer_conv_mixer(x, conv_prev, w_pw1, b_pw1, dw_w, dw_b, ln_g, ln_b, w_pw2, b_pw2):
    a, g = jnp.split(x @ w_pw1 + b_pw1, 2, axis=-1)
    u = a * jax.nn.sigmoid(g)
    u, conv_state = causal_dwconv(u, conv_prev, dw_w, dw_b)
    u = layer_norm(u, ln_g, ln_b)
    return jax.nn.silu(u) @ w_pw2 + b_pw2, conv_state


def ffn(x, w1, w2):
    return jnp.square(jax.nn.relu(x @ w1)) @ w2


def rope(x, pos):
    half = HEAD_DIM // 2
    inv_freq = ROPE_THETA ** (-jnp.arange(half, dtype=jnp.float32) / half)
    ang = pos.astype(jnp.float32)[:, None] * inv_freq
    cos = jnp.cos(ang)[None, :, None, None, :]
    sin = jnp.sin(ang)[None, :, None, None, :]
    xf = x.astype(jnp.float32)
    x1, x2 = xf[..., :half], xf[..., half:]
    return jnp.concatenate([x1 * cos - x2 * sin, x2 * cos + x1 * sin], axis=-1).astype(x.dtype)


def qkv_groups(x, w_qkv, pos):
    B, T, _ = x.shape
    qkv = (x @ w_qkv).reshape(B, T, N_GROUPS, 3, ATT_HEADS, HEAD_DIM)
    return rope(qkv[:, :, :, 0], pos), rope(qkv[:, :, :, 1], pos), qkv[:, :, :, 2]


def dilated_attn_prompt(q, k, v, dil, n_back):
    B, T, H, hd = q.shape
    span = dil * ATT_BLOCK
    Tp = -(-T // span) * span
    nb = Tp // span

    def to_blocks(z):
        z = jnp.pad(z, ((0, 0), (0, Tp - T), (0, 0), (0, 0)))
        z = z.reshape(B, Tp // dil, dil, H, hd).transpose(0, 2, 1, 3, 4)
        return z.reshape(B, dil, nb, ATT_BLOCK, H, hd)

    def with_prev(z):
        prev = jnp.pad(z[:, :, :-1], ((0, 0), (0, 0), (1, 0), (0, 0), (0, 0), (0, 0)))
        return jnp.concatenate([prev, z], axis=3)

    qb = to_blocks(q)
    kk = with_prev(to_blocks(k))
    vv = with_prev(to_blocks(v))
    s = jnp.einsum('brnqhd,brnkhd->brnhqk', qb, kk, preferred_element_type=jnp.float32) * (HEAD_DIM ** -0.5)
    qi = jnp.arange(ATT_BLOCK)[:, None]
    kj = jnp.arange(2 * ATT_BLOCK)[None, :]
    dist = qi + ATT_BLOCK - kj
    band = (dist >= 0) & (dist <= n_back)
    valid = band[None] & ((jnp.arange(nb)[:, None, None] > 0) | (kj >= ATT_BLOCK)[None])
    s = jnp.where(valid[None, None, :, None], s, -jnp.inf)
    lse = jax.nn.logsumexp(s, axis=-1)
    p = jnp.exp(s - lse[..., None])
    o = jnp.einsum('brnhqk,brnkhd->brnqhd', p.astype(v.dtype), vv)
    o = o.reshape(B, dil, Tp // dil, H, hd).transpose(0, 2, 1, 3, 4).reshape(B, Tp, H, hd)[:, :T]
    lse = lse.transpose(0, 1, 2, 4, 3).reshape(B, dil, Tp // dil, H).transpose(0, 2, 1, 3).reshape(B, Tp, H)[:, :T]
    return o, lse


def dilated_attn_sample(q, k_ext, v_ext, dil, n_back, w_buf):
    S = q.shape[1]
    idx = w_buf + jnp.arange(S)[:, None] - dil * jnp.arange(n_back + 1)[None, :]
    valid = idx >= 0
    idx = jnp.maximum(idx, 0)
    kg = k_ext[:, idx]
    vg = v_ext[:, idx]
    s = jnp.einsum('bqhd,bqkhd->bqhk', q, kg, preferred_element_type=jnp.float32) * (HEAD_DIM ** -0.5)
    s = jnp.where(valid[None, :, None, :], s, -jnp.inf)
    lse = jax.nn.logsumexp(s, axis=-1)
    p = jnp.exp(s - lse[..., None])
    o = jnp.einsum('bqhk,bqkhd->bqhd', p.astype(v_ext.dtype), vg)
    return o, lse


def merge_groups(outs, lses, dtype):
    w = jax.nn.softmax(jnp.stack(lses, axis=0), axis=0)
    o = jnp.einsum('gbth,gbthd->bthd', w, jnp.stack(outs, axis=0).astype(jnp.float32))
    return o.reshape(o.shape[0], o.shape[1], -1).astype(dtype)


def attn_prompt(x, w_qkv, w_o):
    B, T, _ = x.shape
    q, k, v = qkv_groups(x, w_qkv, jnp.arange(T))
    outs, lses, kv_state = [], [], []
    for g, (win, dil) in enumerate(ATT_GROUPS):
        o, l = dilated_attn_prompt(q[:, :, g], k[:, :, g], v[:, :, g], dil, win // dil)
        outs.append(o)
        lses.append(l)
        keep = min(win, T)
        kv_state.append(jnp.stack([k[:, T - keep:, g], v[:, T - keep:, g]], axis=2))
    return merge_groups(outs, lses, x.dtype) @ w_o, kv_state


def attn_sample(x, kv_bufs, w_qkv, w_o):
    S = x.shape[1]
    q, k, v = qkv_groups(x, w_qkv, PAST_LEN + jnp.arange(S))
    outs, lses, kv_new = [], [], []
    for g, (win, dil) in enumerate(ATT_GROUPS):
        buf = kv_bufs[g]
        k_ext = jnp.concatenate([buf[:, :, 0].astype(k.dtype), k[:, :, g]], axis=1)
        v_ext = jnp.concatenate([buf[:, :, 1].astype(v.dtype), v[:, :, g]], axis=1)
        o, l = dilated_attn_sample(q[:, :, g], k_ext, v_ext, dil, win // dil, buf.shape[1])
        outs.append(o)
        lses.append(l)
        kv_new.append(jnp.stack([k[:, :, g], v[:, :, g]], axis=2).astype(buf.dtype))
    return merge_groups(outs, lses, x.dtype) @ w_o, kv_new


def setup_inputs(seed: int = 0) -> dict:
    key = jax.random.key(seed)
    ks = iter(jax.random.split(key, 40))

    def nrm(shape, scale=1.0):
        return scale * jax.random.normal(next(ks), shape, jnp.float32)

    inp = {}
    inp['x_prompt'] = nrm((BATCH, SEQ, D_MODEL))
    inp['x_sample'] = nrm((DEC_BATCH, DEC_SEQ, D_MODEL))
    inp['state_lru_conv'] = nrm((N_LRU_LAYERS, DEC_BATCH, LRU_CONV_W - 1, D_RNN))
    inp['state_lru_h'] = nrm((N_LRU_LAYERS, DEC_BATCH, D_RNN), 0.5)
    inp['state_cm_conv'] = nrm((N_CONV_LAYERS, DEC_BATCH, CM_KERNEL - 1, D_CONV), 0.5)
    inp['cache_kv_w128'] = nrm((N_ATTN_LAYERS, DEC_BATCH, min(ATT_GROUPS[0][0], PAST_LEN), 2, ATT_HEADS, HEAD_DIM))
    inp['cache_kv_w512'] = nrm((N_ATTN_LAYERS, DEC_BATCH, min(ATT_GROUPS[1][0], PAST_LEN), 2, ATT_HEADS, HEAD_DIM))
    inp['cache_kv_w2048'] = nrm((N_ATTN_LAYERS, DEC_BATCH, min(ATT_GROUPS[2][0], PAST_LEN), 2, ATT_HEADS, HEAD_DIM))
    inp['norm_mix'] = 1.0 + nrm((DEPTH, D_MODEL), 0.02)
    inp['norm_ffn'] = 1.0 + nrm((DEPTH, D_MODEL), 0.02)
    inp['norm_final'] = 1.0 + nrm((D_MODEL,), 0.02)
    inp['lru_w_in'] = nrm((N_LRU_LAYERS, D_MODEL, 2 * D_RNN), D_MODEL ** -0.5)
    inp['lru_conv_w'] = nrm((N_LRU_LAYERS, LRU_CONV_W, D_RNN), LRU_CONV_W ** -0.5)
    inp['lru_conv_b'] = nrm((N_LRU_LAYERS, D_RNN), 0.01)
    inp['lru_gate_a_w'] = nrm((N_LRU_LAYERS, LRU_BLOCKS, LRU_BLOCK_DIM, LRU_BLOCK_DIM), LRU_BLOCK_DIM ** -0.5)
    inp['lru_gate_a_b'] = nrm((N_LRU_LAYERS, D_RNN), 0.01)
    inp['lru_gate_x_w'] = nrm((N_LRU_LAYERS, LRU_BLOCKS, LRU_BLOCK_DIM, LRU_BLOCK_DIM), LRU_BLOCK_DIM ** -0.5)
    inp['lru_gate_x_b'] = nrm((N_LRU_LAYERS, D_RNN), 0.01)
    a0 = jax.random.uniform(next(ks), (N_LRU_LAYERS, D_RNN), jnp.float32, 0.9, 0.999)
    s0 = a0 ** (1.0 / LRU_C)
    inp['lru_lambda'] = jnp.log(s0) - jnp.log1p(-s0)
    inp['lru_w_out'] = nrm((N_LRU_LAYERS, D_RNN, D_MODEL), D_RNN ** -0.5)
    inp['cm_w_pw1'] = nrm((N_CONV_LAYERS, D_MODEL, 2 * D_CONV), D_MODEL ** -0.5)
    inp['cm_b_pw1'] = nrm((N_CONV_LAYERS, 2 * D_CONV), 0.01)
    inp['cm_dw_w'] = nrm((N_CONV_LAYERS, CM_KERNEL, D_CONV), CM_KERNEL ** -0.5)
    inp['cm_dw_b'] = nrm((N_CONV_LAYERS, D_CONV), 0.01)
    inp['cm_ln_g'] = 1.0 + nrm((N_CONV_LAYERS, D_CONV), 0.02)
    inp['cm_ln_b'] = nrm((N_CONV_LAYERS, D_CONV), 0.01)
    inp['cm_w_pw2'] = nrm((N_CONV_LAYERS, D_CONV, D_MODEL), D_CONV ** -0.5)
    inp['cm_b_pw2'] = nrm((N_CONV_LAYERS, D_MODEL), 0.01)
    inp['att_w_qkv'] = nrm((N_ATTN_LAYERS, D_MODEL, N_GROUPS * 3 * ATT_HEADS * HEAD_DIM), D_MODEL ** -0.5)
    inp['att_w_o'] = nrm((N_ATTN_LAYERS, ATT_HEADS * HEAD_DIM, D_MODEL), (ATT_HEADS * HEAD_DIM) ** -0.5)
    inp['ffn_w1'] = nrm((DEPTH, D_MODEL, D_FF), D_MODEL ** -0.5)
    inp['ffn_w2'] = nrm((DEPTH, D_FF, D_MODEL), D_FF ** -0.5)
    return inp


def reference(x_prompt, x_sample, state_lru_conv, state_lru_h, state_cm_conv,
              cache_kv_w128, cache_kv_w512, cache_kv_w2048,
              norm_mix, norm_ffn, norm_final,
              lru_w_in, lru_conv_w, lru_conv_b, lru_gate_a_w, lru_gate_a_b,
              lru_gate_x_w, lru_gate_x_b, lru_lambda, lru_w_out,
              cm_w_pw1, cm_b_pw1, cm_dw_w, cm_dw_b, cm_ln_g, cm_ln_b, cm_w_pw2, cm_b_pw2,
              att_w_qkv, att_w_o, ffn_w1, ffn_w2):
    xp, xs = x_prompt, x_sample
    B = xp.shape[0]
    kv_caches = (cache_kv_w128, cache_kv_w512, cache_kv_w2048)
    p_lru_conv, p_lru_h, s_lru_conv, s_lru_h = [], [], [], []
    p_cm_conv, s_cm_conv = [], []
    p_kv = [[] for _ in ATT_GROUPS]
    s_kv = [[] for _ in ATT_GROUPS]
    for i in range(DEPTH):
        m, j = i % N_MIXERS, i // N_MIXERS
        up, us = rms_norm(xp, norm_mix[i]), rms_norm(xs, norm_mix[i])
        if m == 0:
            lp = (lru_w_in[j], lru_conv_w[j], lru_conv_b[j], lru_gate_a_w[j], lru_gate_a_b[j],
                  lru_gate_x_w[j], lru_gate_x_b[j], lru_lambda[j], lru_w_out[j])
            yp, cp, hp = rglru_mixer(up, jnp.zeros((B, LRU_CONV_W - 1, D_RNN), xp.dtype),
                                     jnp.zeros((B, D_RNN), state_lru_h.dtype), *lp)
            ys, cs, hs = rglru_mixer(us, state_lru_conv[j], state_lru_h[j], *lp)
            p_lru_conv.append(cp)
            p_lru_h.append(hp)
            s_lru_conv.append(cs.astype(state_lru_conv.dtype))
            s_lru_h.append(hs)
        elif m == 1:
            cpar = (cm_w_pw1[j], cm_b_pw1[j], cm_dw_w[j], cm_dw_b[j], cm_ln_g[j], cm_ln_b[j], cm_w_pw2[j], cm_b_pw2[j])
            yp, cp = conformer_conv_mixer(up, jnp.zeros((B, CM_KERNEL - 1, D_CONV), xp.dtype), *cpar)
            ys, cs = conformer_conv_mixer(us, state_cm_conv[j], *cpar)
            p_cm_conv.append(cp)
            s_cm_conv.append(cs.astype(state_cm_conv.dtype))
        else:
            yp, kvp = attn_prompt(up, att_w_qkv[j], att_w_o[j])
            ys, kvs = attn_sample(us, tuple(c[j] for c in kv_caches), att_w_qkv[j], att_w_o[j])
            for g in range(N_GROUPS):
                p_kv[g].append(kvp[g])
                s_kv[g].append(kvs[g])
        xp = xp + yp
        xs = xs + ys
        xp = xp + ffn(rms_norm(xp, norm_ffn[i]), ffn_w1[i], ffn_w2[i])
        xs = xs + ffn(rms_norm(xs, norm_ffn[i]), ffn_w1[i], ffn_w2[i])
    y_prompt = rms_norm(xp, norm_final)
    y_sample = rms_norm(xs, norm_final)
    return (y_prompt, y_sample,
            jnp.stack(p_lru_conv), jnp.stack(p_lru_h), jnp.stack(p_cm_conv),
            jnp.stack(p_kv[0]), jnp.stack(p_kv[1]), jnp.stack(p_kv[2]),
            jnp.stack(s_lru_conv), jnp.stack(s_lru_h), jnp.stack(s_cm_conv),
            jnp.stack(s_kv[0]), jnp.stack(s_kv[1]), jnp.stack(s_kv[2]))
```

```python
import os
import numpy as np
from contextlib import ExitStack
import concourse.bass as bass
import concourse.mybir as mybir
from concourse.bass_utils import run_bass_kernel_spmd

F32 = mybir.dt.float32
BF16 = mybir.dt.bfloat16
AF = mybir.ActivationFunctionType
ALU = mybir.AluOpType
AX = mybir.AxisListType

D = 1024
DFF = 4096
SEQ = 4096
NSAMP = 16
DSEQ = 4
DRNN = 1408
NB = 16
BD = 88
EPS = 1e-6
PAST = 2048
SAME_ENGINE_SYNC = os.environ.get("SES", "1") == "1"
NSLOT = 8
ROPE_ENG = os.environ.get('ROPE_ENG', 'dve')


class Buf:
    __slots__ = ("name", "w", "r")

    def __init__(self, name=""):
        self.name = name
        self.w = []
        self.r = {}


class Sched:
    ENGS = ("pe", "act", "dve", "pool", "sp")

    def __init__(self, nc, es):
        self.nc = nc
        self.sems = {}
        self.val = {}
        for e in self.ENGS:
            self.sems[e] = es.enter_context(nc.semaphore("s_" + e))
            self.val[e] = 0
        for q in ("sp", "pool", "act"):
            for i in range(NSLOT):
                k = "d_%s_%d" % (q, i)
                self.sems[k] = es.enter_context(nc.semaphore(k))
                self.val[k] = 0
        self.known = {e: {} for e in self.ENGS}
        self.ops = {e: [] for e in self.ENGS}
        self.dma_k = {"sp": 0, "pool": 0, "act": 0}
        self.n_ops = 0

    def _waits(self, eng, deps):
        need = {}
        kn = self.known[eng]
        for (k, v) in deps:
            if k == eng and (eng == "pe" or not SAME_ENGINE_SYNC):
                continue
            if kn.get(k, 0) >= v:
                continue
            if need.get(k, 0) < v:
                need[k] = v
        for k, v in need.items():
            kn[k] = v
        return list(need.items())

    @staticmethod
    def _deps(reads, writes):
        deps = []
        for b in reads:
            deps += b.w
        for b in writes:
            deps += b.w
            deps += list(b.r.items())
        return deps

    @staticmethod
    def _mark(tok, reads, writes):
        for b in reads:
            if b.r.get(tok[0], 0) < tok[1]:
                b.r[tok[0]] = tok[1]
        for b in writes:
            b.w = [tok]
            b.r = {}

    def op(self, eng, fn, reads=(), writes=()):
        waits = self._waits(eng, self._deps(reads, writes))
        self.val[eng] += 1
        tok = (eng, self.val[eng])
        self.ops[eng].append((waits, fn, eng, 1))
        self._mark(tok, reads, writes)
        self.n_ops += 1
        return tok

    def dma(self, q, out, in_, reads=(), writes=(), slow=False):
        k = "d_%s_%d" % (q, self.dma_k[q] % NSLOT)
        self.dma_k[q] += 1
        deps = []
        for b in reads:
            deps += b.w
        for b in writes:
            deps += [t for t in b.w if not t[0].startswith("d_")]
            deps += list(b.r.items())
        if self.val[k] > 0:
            deps.append((k, self.val[k]))
        waits = self._waits(q, deps)
        self.val[k] += 16
        tok = (k, self.val[k])
        keep = {id(b): [t for t in b.w if t[0].startswith("d_") and t[0] != k] for b in writes}

        def fn(eng, out=out, in_=in_, slow=slow):
            if slow:
                return eng.dma_start(out=out, in_=in_, allow_slow_non_contiguous=True)
            return eng.dma_start(out=out, in_=in_)
        self.ops[q].append((waits, fn, k, 16))
        self._mark(tok, reads, writes)
        for b in writes:
            b.w = keep[id(b)] + [tok]
        self.n_ops += 1
        return tok

    def barrier(self):
        deps = [(k, v) for k, v in self.val.items() if v > 0]
        for e in self.ENGS:
            waits = self._waits(e, [d for d in deps if d[0] != e])
            if waits:
                self.ops[e].append((waits, None, None, 0))

    def flush(self):
        nc = self.nc
        amap = {"pe": "tensor", "act": "scalar", "dve": "vector", "pool": "gpsimd", "sp": "sync"}
        with nc.Block() as block:
            for e in self.ENGS:
                lst = self.ops[e]
                if not lst:
                    continue

                def body(eng, lst=lst):
                    for (waits, fn, ikey, iamt) in lst:
                        for (k, v) in waits:
                            eng.wait_ge(self.sems[k], v)
                        if fn is not None:
                            ins = fn(eng)
                            ins.then_inc(self.sems[ikey], iamt)
                getattr(block, amap[e])(body)
        self.ops = {e: [] for e in self.ENGS}


class Ctx:
    pass


LRU_DEPTH = int(os.environ.get("LRU_DEPTH", "6"))


def interleave(gens, depth):
    active = []
    it = iter(gens)
    more = True
    while True:
        while more and len(active) < depth:
            try:
                active.append(next(it))
            except StopIteration:
                more = False
        if not active:
            break
        for g in list(active):
            try:
                next(g)
            except StopIteration:
                active.remove(g)


def build(n_ptiles=32, phases=("ffn",), depth=4):
    nc = bass.Bass("TRN2", target_bir_lowering=False)
    NT = n_ptiles + 1
    T_P = n_ptiles * 128
    c = Ctx()
    c.nc = nc
    c.NT = NT

    def din(name, shape, dt=F32):
        return nc.dram_tensor(name, list(shape), dt, kind="ExternalInput").ap()

    def dout(name, shape, dt=F32):
        return nc.dram_tensor(name, list(shape), dt, kind="ExternalOutput").ap()

    def dscratch(name, shape, dt=F32):
        return nc.dram_tensor(name, list(shape), dt, kind="Internal").ap()

    x_in = din("x_in", [NT * 128, D])
    norm_mix = din("norm_mix", [4, D])
    norm_ffn = din("norm_ffn", [4, D])
    norm_final = din("norm_final", [1, D])
    ffn_w1 = din("ffn_w1", [4, D, DFF])
    ffn_w2 = din("ffn_w2", [4, DFF, D])
    ident_d = din("ident", [128, 128])
    y_out = dout("y_out", [NT * 128, D])
    lru_w_in = din("lru_w_in", [2, D, 2 * DRNN])
    lru_conv_w = din("lru_conv_w", [2, 4, DRNN])
    lru_conv_b = din("lru_conv_b", [2, DRNN])
    lru_ga_w = din("lru_gate_a_w", [2, NB, BD, BD])
    lru_ga_b = din("lru_gate_a_b", [2, DRNN])
    lru_gx_w = din("lru_gate_x_w", [2, NB, BD, BD])
    lru_gx_b = din("lru_gate_x_b", [2, DRNN])
    lru_lam = din("lru_lambda", [2, DRNN])
    lru_w_out = din("lru_w_out", [2, DRNN, D])
    att_w_qkv = din("att_w_qkv", [1, D, 9 * D])
    att_w_o = din("att_w_o", [1, D, D])
    rope_tab = din("rope_tab", [NT * 128, 1024])
    amask_d = din("amask", [128, 512])
    smask_d = din("smask", [128, 4, 32])
    KEEP = [min(128, T_P), min(512, T_P), min(2048, T_P)]
    o_pkv = [dout("p_kv%d" % g, [1, KEEP[g], 2, 8, 128]) for g in range(3)]
    o_skv = [dout("s_kv%d" % g, [64, 2, 8, 128]) for g in range(3)]
    cache_kv = [din("cache_kv%d" % g, [NSAMP, [128, 512, 512][g], 2, 8, 128]) for g in range(3)]
    qkv_s = dscratch("qkv_s", [NT * 128, 9 * D], BF16)
    og_s = [dscratch("og_s%d" % g, [T_P + 16, 8 * 129]) for g in range(3)]
    osamp_s = dscratch("osamp_s", [64, 8 * 129])
    cm_w_pw1 = din("cm_w_pw1", [1, D, 2 * D])
    cm_b_pw1 = din("cm_b_pw1", [1, 2 * D])
    cm_dw_w = din("cm_dw_w", [1, 31, D])
    cm_dw_b = din("cm_dw_b", [1, D])
    cm_ln_g = din("cm_ln_g", [1, D])
    cm_ln_b = din("cm_ln_b", [1, D])
    cm_w_pw2 = din("cm_w_pw2", [1, D, D])
    cm_b_pw2 = din("cm_b_pw2", [1, D])
    st_cm_conv = din("state_cm_conv", [1, NSAMP, 30, D])
    o_p_cm_conv = dout("p_cm_conv", [1, 30, D])
    o_s_cm_conv = dout("s_cm_conv", [1, NSAMP, 30, D])
    st_lru_conv = din("state_lru_conv", [2, NSAMP, 3, DRNN])
    st_lru_h = din("state_lru_h", [2, NSAMP, DRNN])
    o_p_lru_conv = dout("p_lru_conv", [2, 3, DRNN])
    o_p_lru_h = dout("p_lru_h", [2, DRNN])
    o_s_lru_conv = dout("s_lru_conv", [2, NSAMP, 3, DRNN])
    o_s_lru_h = dout("s_lru_h", [2, NSAMP, DRNN])
    xres = dscratch("xres", [NT * 128, D])

    with ExitStack() as es:
        S = Sched(nc, es)
        c.S = S
        xb = [Buf("x%d" % i) for i in range(NT)]

        c.uid = 0

        def sb(st, name, shape, dt):
            c.uid += 1
            return st.enter_context(nc.sbuf_tensor("%s_%d" % (name, c.uid), list(shape), dt))

        ps = [es.enter_context(nc.psum_tensor("ps%d" % i, [128, 512], F32)) for i in range(8)]
        pb = [Buf("ps%d" % i) for i in range(8)]

        ident_f = sb(es, "ident_f", [128, 128], F32)
        ident_b = sb(es, "ident_b", [128, 128], BF16)
        eps_t = sb(es, "eps_t", [128, 1], F32)
        cb = Buf("consts")
        S.dma("sp", ident_f[:], ident_d[:, :], writes=[cb])
        S.op("dve", lambda e: e.tensor_copy(out=ident_b[:], in_=ident_f[:]), reads=[cb], writes=[cb])
        S.op("dve", lambda e: e.memset(eps_t[:], EPS), writes=[cb])

        def load_w_bf16(st, name, src2d, K, N, q="pool", kp=128):
            kc = K // kp
            t = sb(st, name, [kp, kc, N], BF16)
            b = Buf(name)
            src = src2d.rearrange("(kc p) n -> p kc n", p=kp)
            step = max(1, (1 << 21) // (N * 4 * kp))
            for k0 in range(0, kc, step):
                k1 = min(kc, k0 + step)
                S.dma(q, t[:, k0:k1, :], src[:, k0:k1, :], writes=[b])
            return t, b

        def load_bcast(st, name, src_row, N, q="sp"):
            t = sb(st, name, [128, N], F32)
            b = Buf(name)
            S.dma(q, t[:], src_row.partition_broadcast(128), writes=[b])
            return t, b

        c.rr = 0

        def norm_tile(xt, xtb, g_t, g_b, xn, xnb, junk, junkb, st1, st1b):
            S.op("act", lambda e: e.activation(out=junk[:], in_=xt[:], func=AF.Square, accum_out=st1[:, 0:1]),
                 reads=[xtb], writes=[junkb, st1b])
            S.op("act", lambda e: e.activation(out=st1[:, 1:2], in_=st1[:, 0:1], func=AF.Sqrt,
                                               scale=1.0 / D, bias=eps_t[:, 0:1]),
                 reads=[st1b, cb], writes=[st1b])
            S.op("dve", lambda e: e.reciprocal(out=st1[:, 2:3], in_=st1[:, 1:2]), reads=[st1b], writes=[st1b])
            S.op("dve", lambda e: e.scalar_tensor_tensor(out=xn[:], in0=xt[:], scalar=st1[:, 2:3], in1=g_t[:],
                                                         op0=ALU.mult, op1=ALU.mult),
                 reads=[xtb, st1b, g_b], writes=[xnb])

        def transpose_to(xn, xnb, nchunk, bank, dst_fn, dstb, csz=128, rows=128):
            assert nchunk == 8 and csz == 128 and rows == 128
            pbf = ps[bank][:].bitcast(BF16)

            def fn(e):
                ins = None
                for ci in range(8):
                    ins = e.transpose(out=pbf[:, ci * 128:(ci + 1) * 128], in_=xn[:, ci * 128:(ci + 1) * 128],
                                      identity=ident_b[:, :])
                return ins
            S.op("pe", fn, reads=[xnb, cb], writes=[pb[bank]])
            S.op("act", lambda e: e.copy(out=dst_fn(slice(0, 4)), in_=pbf[:, 0:512].rearrange("p (c r) -> p c r", r=128)),
                 reads=[pb[bank]], writes=[dstb])
            S.op("dve", lambda e: e.tensor_copy(out=dst_fn(slice(4, 8)),
                                                 in_=pbf[:, 512:1024].rearrange("p (c r) -> p c r", r=128)),
                 reads=[pb[bank]], writes=[dstb])

        c.sb = sb
        c.ps = ps
        c.pb = pb
        c.xb = xb

        def phase_init():
            with ExitStack() as st:
                for ti in range(NT):
                    S.dma("sp", xres[ti * 128:(ti + 1) * 128, :],
                          x_in[ti * 128:(ti + 1) * 128, :], writes=[xb[ti]])
                S.barrier()
                S.flush()

        def phase_ffn(li):
            with ExitStack() as st:
                w1, w1b = load_w_bf16(st, "w1", ffn_w1[li], D, DFF)
                w2, w2b = load_w_bf16(st, "w2", ffn_w2[li], DFF, D)
                g_t, g_b = load_bcast(st, "g_ffn", norm_ffn[li:li + 1, :], D)
                GT = 2
                xt = [sb(st, "xt%d" % i, [128, D], F32) for i in range(2 * GT)]
                xtb = [Buf() for _ in range(2 * GT)]
                xn = [sb(st, "xn%d" % i, [128, D], BF16) for i in range(2)]
                xnb = [Buf() for _ in range(2)]
                junk = sb(st, "junk", [128, D], BF16)
                junkb = Buf()
                st1 = [sb(st, "st%d" % i, [128, 4], F32) for i in range(2)]
                st1b = [Buf() for _ in range(2)]
                xnT = [sb(st, "xnT%d" % i, [128, 8, GT * 128], BF16) for i in range(2)]
                xnTb = [Buf() for _ in range(2)]
                hT = sb(st, "hT", [128, 32, GT * 128], BF16)
                hTb = Buf()
                sq = [sb(st, "sq%d" % i, [128, 2 * GT * 128], F32) for i in range(2)]
                sqb = [Buf() for _ in range(2)]
                groups = [list(range(g, min(g + GT, n_ptiles))) for g in range(0, n_ptiles, GT)] + [[n_ptiles]]
                for gi, tiles in enumerate(groups):
                    W = len(tiles) * 128
                    xs = gi % 2
                    for j, ti in enumerate(tiles):
                        slot = xs * GT + j
                        S.dma("sp", xt[slot][:], xres[ti * 128:(ti + 1) * 128, :], reads=[xb[ti]], writes=[xtb[slot]])
                        nsl = c.rr % 2
                        c.rr += 1
                        norm_tile(xt[slot], xtb[slot], g_t, g_b, xn[nsl], xnb[nsl], junk, junkb, st1[nsl], st1b[nsl])
                        transpose_to(xn[nsl], xnb[nsl], 8, 0,
                                     lambda ci, j=j, xs=xs: xnT[xs][:, ci, j * 128:(j + 1) * 128], xnTb[xs])
                    for jp in range(16):
                        bank = 1 + (jp % 3)

                        def mm1(e, jp=jp, bank=bank, xs=xs, W=W):
                            ins = None
                            for hi in range(2):
                                jc = 2 * jp + hi
                                for kc in range(8):
                                    ins = e.matmul(ps[bank][:, hi * 256:hi * 256 + W], lhsT=w1[:, kc, jc * 128:(jc + 1) * 128],
                                                   rhs=xnT[xs][:, kc, 0:W], start=(kc == 0 and hi == 0), stop=(kc == 7))
                            return ins
                        S.op("pe", mm1, reads=[w1b, xnTb[xs]], writes=[pb[bank]])
                        sl = jp % 2
                        psv = ps[bank][:, :].rearrange("p (a w) -> p a w", a=2)[:, :, 0:W]
                        sqv = sq[sl][:, :].rearrange("p (a w) -> p a w", a=2)[:, :, 0:W]
                        S.op("act", lambda e, psv=psv, sqv=sqv: e.activation(out=sqv, in_=psv, func=AF.Square),
                             reads=[pb[bank]], writes=[sqb[sl]])
                        S.op("dve", lambda e, psv=psv, sqv=sqv, jp=jp, W=W: e.scalar_tensor_tensor(
                            out=hT[:, 2 * jp:2 * jp + 2, 0:W], in0=psv, scalar=0.0, in1=sqv,
                            op0=ALU.is_gt, op1=ALU.mult), reads=[pb[bank], sqb[sl]], writes=[hTb])
                    for j, ti in enumerate(tiles):
                        slot = xs * GT + j
                        for half in range(2):
                            bank = 4 + ((2 * j + half) % 4)

                            def mm2(e, j=j, half=half, bank=bank):
                                ins = None
                                for jc in range(32):
                                    ins = e.matmul(ps[bank][:, :], lhsT=hT[:, jc, j * 128:(j + 1) * 128],
                                                   rhs=w2[:, jc, half * 512:(half + 1) * 512],
                                                   start=(jc == 0), stop=(jc == 31))
                                return ins
                            S.op("pe", mm2, reads=[w2b, hTb], writes=[pb[bank]])
                            S.op("dve", lambda e, slot=slot, half=half, bank=bank: e.tensor_tensor(
                                out=xt[slot][:, half * 512:(half + 1) * 512], in0=ps[bank][:, :],
                                in1=xt[slot][:, half * 512:(half + 1) * 512], op=ALU.add),
                                reads=[pb[bank], xtb[slot]], writes=[xtb[slot]])
                        S.dma("sp", xres[ti * 128:(ti + 1) * 128, :], xt[slot][:], reads=[xtb[slot]], writes=[xb[ti]])
                S.barrier()
                S.flush()


        def phase_lru(li, j):
            with ExitStack() as st:
                w_in, w_inb = load_w_bf16(st, "w_in", lru_w_in[j], D, 2 * DRNN)
                w_out, w_outb = load_w_bf16(st, "w_out", lru_w_out[j], DRNN, D, kp=BD)
                gaw = sb(st, "gaw", [BD, NB, BD], BF16)
                gxw = sb(st, "gxw", [BD, NB, BD], BF16)
                gwb = Buf()
                S.dma("pool", gaw[:], lru_ga_w[j].rearrange("n c d -> c n d"), writes=[gwb])
                S.dma("pool", gxw[:], lru_gx_w[j].rearrange("n c d -> c n d"), writes=[gwb])
                g_t, g_b = load_bcast(st, "g_mix", norm_mix[li:li + 1, :], D)
                prm = sb(st, "lprm", [BD, 12, NB], F32)
                prmb = Buf()
                ones = sb(st, "lones", [128, 1], F32)
                S.op("dve", lambda e: e.memset(ones[:], 1.0), writes=[prmb])
                S.dma("sp", prm[:, 0:4, :], lru_conv_w[j].rearrange("k (n p) -> p k n", p=BD), writes=[prmb], slow=True)
                for idx, src in ((4, lru_conv_b), (5, lru_ga_b), (6, lru_gx_b), (7, lru_lam)):
                    S.dma("sp", prm[:, idx, :], src[j].rearrange("(n p) -> p n", p=BD), writes=[prmb], slow=True)
                S.op("act", lambda e: e.activation(out=prm[:, 9, :], in_=prm[:, 7, :], func=AF.Exp, scale=-1.0),
                     reads=[prmb], writes=[prmb])
                S.op("act", lambda e: e.activation(out=prm[:, 8, :], in_=prm[:, 9, :], func=AF.Ln, bias=ones[0:BD, 0:1]),
                     reads=[prmb], writes=[prmb])
                S.op("dve", lambda e: e.tensor_scalar(out=prm[:, 8, :], in0=prm[:, 8, :], scalar1=-8.0, scalar2=None,
                                                      op0=ALU.mult), reads=[prmb], writes=[prmb])
                GT = 2
                WM = GT * 128
                xt = [sb(st, "lxt%d" % i, [128, D], F32) for i in range(2 * GT)]
                xtb = [Buf() for _ in range(2 * GT)]
                xn = [sb(st, "lxn%d" % i, [128, D], BF16) for i in range(2)]
                xnb = [Buf() for _ in range(2)]
                junk = sb(st, "ljunk", [128, D], BF16)
                junkb = Buf()
                st1 = [sb(st, "lst%d" % i, [128, 4], F32) for i in range(2)]
                st1b = [Buf() for _ in range(2)]
                xnT = [sb(st, "lxnT%d" % i, [128, 8, WM], BF16) for i in range(2)]
                xnTb = [Buf() for _ in range(2)]
                ubuf = sb(st, "ubuf", [BD, NB, 3 + WM], F32)
                ubufb = [Buf() for _ in range(NB)]
                ubs = sb(st, "ubs", [BD, NB, 7, NSAMP], F32)
                ubsb = [Buf() for _ in range(NB)]
                hst = sb(st, "hst", [BD, NB], F32)
                hstb = [Buf() for _ in range(NB)]
                hss = sb(st, "hss", [BD, NB, NSAMP], F32)
                hssb = [Buf() for _ in range(NB)]
                yT = sb(st, "yT", [BD, NB, WM], BF16)
                yTb = Buf()
                NW = 8
                wk = {}
                wkb = {}
                for nm, dt_ in (("gg", F32), ("gt", F32), ("A", F32), ("B", F32), ("uc", F32), ("a", F32), ("ucb", BF16)):
                    wk[nm] = [sb(st, "l" + nm + str(i), [BD, WM], dt_) for i in range(NW)]
                    wkb[nm] = [Buf() for _ in range(NW)]
                for alias, phys in (("t1", "A"), ("r", "A"), ("a2", "A"), ("h", "A"), ("t2", "B"), ("i", "B"), ("b", "B")):
                    wk[alias] = wk[phys]
                    wkb[alias] = wkb[phys]
                S.op("dve", lambda e: e.memset(ubuf[:, :, 0:3], 0.0), writes=ubufb)
                S.op("dve", lambda e: e.memset(hst[:], 0.0), writes=hstb)
                S.op("pool", lambda e: e.memset(yT[:], 0.0), writes=[yTb])
                for n in range(NB):
                    for kk in range(3):
                        S.dma("sp", ubs[:, n, kk, :], st_lru_conv[j][:, kk, n * BD:(n + 1) * BD].rearrange("b p -> p b"),
                              writes=[ubsb[n]], slow=True)
                    S.dma("sp", hss[:, n, :], st_lru_h[j][:, n * BD:(n + 1) * BD].rearrange("b p -> p b"),
                          writes=[hssb[n]], slow=True)
                groups = [list(range(g, min(g + GT, n_ptiles))) for g in range(0, n_ptiles, GT)] + [[n_ptiles]]
                obank = [0]
                for gi, tiles in enumerate(groups):
                    W = len(tiles) * 128
                    samp = (tiles[0] == n_ptiles)
                    WV = 64 if samp else W
                    xs = gi % 2
                    for jj, ti in enumerate(tiles):
                        slot = xs * GT + jj
                        if li == 0:
                            S.dma("sp", xt[slot][:], x_in[ti * 128:(ti + 1) * 128, :], writes=[xtb[slot]])
                        else:
                            S.dma("sp", xt[slot][:], xres[ti * 128:(ti + 1) * 128, :], reads=[xb[ti]], writes=[xtb[slot]])
                        nsl = c.rr % 2
                        c.rr += 1
                        norm_tile(xt[slot], xtb[slot], g_t, g_b, xn[nsl], xnb[nsl], junk, junkb, st1[nsl], st1b[nsl])
                        transpose_to(xn[nsl], xnb[nsl], 8, 0,
                                     lambda ci, jj=jj, xs=xs: xnT[xs][:, ci, jj * 128:(jj + 1) * 128], xnTb[xs])
                    def chunk(n, gi=gi, W=W, samp=samp, WV=WV, xs=xs):
                        k = n % NW
                        bank = 1 + (n % 3)
                        gbank = 4 + (n % 2)

                        def mm1(e, n=n, bank=bank, xs=xs, W=W):
                            ins = None
                            for half in range(2):
                                c0 = half * DRNN + n * BD
                                for kc in range(8):
                                    ins = e.matmul(ps[bank][0:BD, half * 256:half * 256 + W], lhsT=w_in[:, kc, c0:c0 + BD],
                                                   rhs=xnT[xs][:, kc, 0:W], start=(kc == 0 and half == 0), stop=(kc == 7))
                            return ins
                        S.op("pe", mm1, reads=[w_inb, xnTb[xs]], writes=[pb[bank]])
                        gps = ps[bank][0:BD, 0:WV]
                        ups = ps[bank][0:BD, 256:256 + WV]
                        if not samp:
                            S.op("act", lambda e, n=n, ups=ups, W=W: e.copy(out=ubuf[:, n, 3:3 + W], in_=ups),
                                 reads=[pb[bank]], writes=[ubufb[n]])
                        else:
                            S.op("act", lambda e, n=n, ups=ups: e.copy(out=ubs[:, n, 3:7, :].rearrange("p s b -> p (s b)"),
                                                                        in_=ups),
                                 reads=[pb[bank]], writes=[ubsb[n]])
                        gt = wk["gt"][k][:, 0:WV]
                        S.op("act", lambda e, gt=gt, gps=gps: e.copy(out=gt, in_=gps), reads=[pb[bank]], writes=[wkb["gt"][k]])
                        yield
                        gps = gt
                        gsrc = wkb["gt"][k]
                        t1 = wk["t1"][k][:, 0:WV]
                        t2 = wk["t2"][k][:, 0:WV]
                        gg = wk["gg"][k][:, 0:WV]
                        S.op("act", lambda e, t1=t1, gps=gps: e.activation(out=t1, in_=gps, func=AF.Square,
                                                                            scale=0.21145921592590745),
                             reads=[gsrc], writes=[wkb["t1"][k]])
                        S.op("dve", lambda e, t1=t1, t2=t2, gps=gps: e.scalar_tensor_tensor(
                            out=t2, in0=t1, scalar=1.0, in1=gps, op0=ALU.add, op1=ALU.mult),
                             reads=[wkb["t1"][k], gsrc], writes=[wkb["t2"][k]])
                        yield
                        S.op("act", lambda e, t2=t2: e.activation(out=t2, in_=t2, func=AF.Sigmoid, scale=1.5957691216057308),
                             reads=[wkb["t2"][k]], writes=[wkb["t2"][k]])
                        S.op("dve", lambda e, gg=gg, t2=t2, gps=gps: e.tensor_tensor(out=gg, in0=t2, in1=gps, op=ALU.mult),
                             reads=[wkb["t2"][k], gsrc], writes=[wkb["gg"][k]])
                        yield
                        uc = wk["uc"][k][:, 0:WV]
                        ucb = wk["ucb"][k][:, 0:WV]
                        if not samp:
                            def win(kk, n=n, W=W):
                                return ubuf[:, n, kk:kk + W]
                            ucv = uc
                            ub_ = ubufb[n]
                        else:
                            def win(kk, n=n):
                                return ubs[:, n, kk:kk + 4, :].rearrange("p s b -> p (s b)")
                            ucv = uc
                            ub_ = ubsb[n]
                        S.op("dve", lambda e, n=n, win=win, ucv=ucv: e.tensor_scalar(
                            out=ucv, in0=win(0), scalar1=prm[:, 0, n:n + 1], scalar2=prm[:, 4, n:n + 1],
                            op0=ALU.mult, op1=ALU.add), reads=[ub_, prmb], writes=[wkb["uc"][k]])
                        for kk in range(1, 4):
                            S.op("dve", lambda e, n=n, kk=kk, win=win, ucv=ucv: e.scalar_tensor_tensor(
                                out=ucv, in0=win(kk), scalar=prm[:, kk, n:n + 1], in1=ucv, op0=ALU.mult, op1=ALU.add),
                                reads=[ub_, prmb, wkb["uc"][k]], writes=[wkb["uc"][k]])
                        yield
                        S.op("act", lambda e, uc=uc, ucb=ucb: e.copy(out=ucb, in_=uc), reads=[wkb["uc"][k]],
                             writes=[wkb["ucb"][k]])
                        if not samp:
                            S.op("act", lambda e, n=n, W=W: e.copy(out=ubuf[:, n, 0:3], in_=ubuf[:, n, W:W + 3]),
                                 reads=[ubufb[n]], writes=[ubufb[n]])
                            if gi == len(groups) - 2:
                                S.dma("sp", o_p_lru_conv[j][:, n * BD:(n + 1) * BD].rearrange("k p -> p k"),
                                      ubuf[:, n, 0:3], reads=[ubufb[n]], slow=True)
                        else:
                            for kk in range(3):
                                S.dma("sp", o_s_lru_conv[j][:, kk, n * BD:(n + 1) * BD].rearrange("b p -> p b"),
                                      ubs[:, n, 4 + kk, :], reads=[ubsb[n]], slow=True)
                        yield
                        def mmg(e, n=n, gbank=gbank, ucb=ucb, WV=WV):
                            e.matmul(ps[gbank][0:BD, 0:WV], lhsT=gaw[:, n, :], rhs=ucb, start=True, stop=True)
                            return e.matmul(ps[gbank][0:BD, 256:256 + WV], lhsT=gxw[:, n, :], rhs=ucb, start=False, stop=True)
                        S.op("pe", mmg, reads=[gwb, wkb["ucb"][k]], writes=[pb[gbank]])
                        r_ = wk["r"][k][:, 0:WV]
                        i_ = wk["i"][k][:, 0:WV]
                        a_ = wk["a"][k][:, 0:WV]
                        a2 = wk["a2"][k][:, 0:WV]
                        b_ = wk["b"][k][:, 0:WV]
                        h_ = wk["h"][k][:, 0:WV]
                        S.op("act", lambda e, n=n, r_=r_, gbank=gbank, WV=WV: e.activation(
                            out=r_, in_=ps[gbank][0:BD, 0:WV], func=AF.Sigmoid, bias=prm[:, 5, n:n + 1]),
                            reads=[pb[gbank], prmb], writes=[wkb["r"][k]])
                        S.op("act", lambda e, n=n, i_=i_, gbank=gbank, WV=WV: e.activation(
                            out=i_, in_=ps[gbank][0:BD, 256:256 + WV], func=AF.Sigmoid, bias=prm[:, 6, n:n + 1]),
                            reads=[pb[gbank], prmb], writes=[wkb["i"][k]])
                        yield
                        S.op("act", lambda e, n=n, r_=r_, a_=a_: e.activation(out=a_, in_=r_, func=AF.Exp,
                                                                              scale=prm[:, 8, n:n + 1]),
                             reads=[wkb["r"][k], prmb], writes=[wkb["a"][k]])
                        yield
                        S.op("act", lambda e, a_=a_, a2=a2: e.activation(out=a2, in_=a_, func=AF.Square),
                             reads=[wkb["a"][k]], writes=[wkb["a2"][k]])
                        S.op("act", lambda e, a2=a2: e.activation(out=a2, in_=a2, func=AF.Sqrt, scale=-1.0,
                                                                  bias=ones[0:BD, 0:1]),
                             reads=[wkb["a2"][k], prmb], writes=[wkb["a2"][k]])
                        yield
                        S.op("dve", lambda e, a2=a2, i_=i_, b_=b_: e.tensor_tensor(out=b_, in0=a2, in1=i_, op=ALU.mult),
                             reads=[wkb["a2"][k], wkb["i"][k]], writes=[wkb["b"][k]])
                        S.op("dve", lambda e, b_=b_, uc=uc: e.tensor_tensor(out=b_, in0=b_, in1=uc, op=ALU.mult),
                             reads=[wkb["b"][k], wkb["uc"][k]], writes=[wkb["b"][k]])
                        if not samp:
                            S.op("dve", lambda e, n=n, a_=a_, b_=b_, h_=h_: e.tensor_tensor_scan(
                                out=h_, data0=a_, data1=b_, initial=hst[:, n:n + 1], op0=ALU.mult, op1=ALU.add),
                                reads=[wkb["a"][k], wkb["b"][k], hstb[n]], writes=[wkb["h"][k]])
                            S.op("act", lambda e, n=n, h_=h_, W=W: e.copy(out=hst[:, n:n + 1], in_=h_[:, W - 1:W]),
                                 reads=[wkb["h"][k]], writes=[hstb[n]])
                            if gi == len(groups) - 2:
                                S.dma("sp", o_p_lru_h[j:j + 1, n * BD:(n + 1) * BD].rearrange("o p -> p o"),
                                      hst[:, n:n + 1], reads=[hstb[n]], slow=True)
                        else:
                            for s_ in range(4):
                                cs_ = slice(s_ * 16, s_ * 16 + 16)
                                prev = hss[:, n, :] if s_ == 0 else h_[:, (s_ - 1) * 16:s_ * 16]
                                S.op("dve", lambda e, a_=a_, h_=h_, prev=prev, cs_=cs_: e.tensor_tensor(
                                    out=h_[:, cs_], in0=a_[:, cs_], in1=prev, op=ALU.mult),
                                    reads=[wkb["a"][k], hssb[n], wkb["h"][k]], writes=[wkb["h"][k]])
                                S.op("dve", lambda e, b_=b_, h_=h_, cs_=cs_: e.tensor_tensor(
                                    out=h_[:, cs_], in0=h_[:, cs_], in1=b_[:, cs_], op=ALU.add),
                                    reads=[wkb["b"][k], wkb["h"][k]], writes=[wkb["h"][k]])
                            S.dma("sp", o_s_lru_h[j][:, n * BD:(n + 1) * BD].rearrange("b p -> p b"),
                                  h_[:, 48:64], reads=[wkb["h"][k]], slow=True)
                        yield
                        S.op("dve", lambda e, n=n, h_=h_, gg=gg, WV=WV: e.tensor_tensor(out=yT[:, n, 0:WV], in0=h_, in1=gg,
                                                                                       op=ALU.mult),
                             reads=[wkb["h"][k], wkb["gg"][k]], writes=[yTb])
                    interleave((chunk(n) for n in range(NB)), LRU_DEPTH)
                    for jj, ti in enumerate(tiles):
                        slot = xs * GT + jj
                        for half in range(2):
                            bank = 6 + (obank[0] % 2)
                            obank[0] += 1

                            def mm2(e, jj=jj, half=half, bank=bank):
                                ins = None
                                for n in range(NB):
                                    ins = e.matmul(ps[bank][:, :], lhsT=yT[:, n, jj * 128:(jj + 1) * 128],
                                                   rhs=w_out[:, n, half * 512:(half + 1) * 512],
                                                   start=(n == 0), stop=(n == NB - 1))
                                return ins
                            S.op("pe", mm2, reads=[w_outb, yTb], writes=[pb[bank]])
                            S.op("dve", lambda e, slot=slot, half=half, bank=bank: e.tensor_tensor(
                                out=xt[slot][:, half * 512:(half + 1) * 512], in0=ps[bank][:, :],
                                in1=xt[slot][:, half * 512:(half + 1) * 512], op=ALU.add),
                                reads=[pb[bank], xtb[slot]], writes=[xtb[slot]])
                        S.dma("sp", xres[ti * 128:(ti + 1) * 128, :], xt[slot][:], reads=[xtb[slot]], writes=[xb[ti]])
                S.barrier()
                S.flush()


        def phase_conv(li, j):
            with ExitStack() as st:
                pw1, pw1b = load_w_bf16(st, "pw1", cm_w_pw1[j], D, 2 * D)
                pw2, pw2b = load_w_bf16(st, "pw2", cm_w_pw2[j], D, D)
                g_t, g_b = load_bcast(st, "g_mixc", norm_mix[li:li + 1, :], D)
                b2_t, b2_b = load_bcast(st, "b_pw2", cm_b_pw2[j:j + 1, :], D)
                prm = sb(st, "cprm", [128, 36 + 31 + 4, 8], F32)
                prmb = Buf()
                S.dma("sp", prm[:, 0, :], cm_b_pw1[j, 0:D].rearrange("(n p) -> p n", p=128), writes=[prmb], slow=True)
                S.dma("sp", prm[:, 1, :], cm_b_pw1[j, D:2 * D].rearrange("(n p) -> p n", p=128), writes=[prmb], slow=True)
                for idx, src in ((2, cm_dw_b), (3, cm_ln_g), (4, cm_ln_b)):
                    S.dma("sp", prm[:, idx, :], src[j].rearrange("(n p) -> p n", p=128), writes=[prmb], slow=True)
                for kk in range(31):
                    S.dma("sp", prm[:, 5 + kk, :], cm_dw_w[j, kk].rearrange("(n p) -> p n", p=128), writes=[prmb], slow=True)
                ones_b = sb(st, "cones", [128, 128], BF16)
                S.op("dve", lambda e: e.memset(ones_b[:], 1.0), writes=[prmb])
                dg = sb(st, "dg", [128, 8, 31, 128], BF16)
                dgb = Buf()
                for ch in range(8):
                    for kk in range(31):
                        S.op("pool", lambda e, ch=ch, kk=kk: e.tensor_scalar(
                            out=dg[:, ch, kk, :], in0=ident_f[:], scalar1=prm[:, 5 + kk, ch:ch + 1], scalar2=None,
                            op0=ALU.mult), reads=[prmb, cb], writes=[dgb])
                GT = 2
                WM = GT * 128
                xt = [sb(st, "cxt%d" % i, [128, D], F32) for i in range(2 * GT)]
                xtb = [Buf() for _ in range(2 * GT)]
                xn = [sb(st, "cxn%d" % i, [128, D], BF16) for i in range(2)]
                xnb = [Buf() for _ in range(2)]
                junk = sb(st, "cjunk", [128, D], BF16)
                junkb = Buf()
                st1 = [sb(st, "cst%d" % i, [128, 4], F32) for i in range(2)]
                st1b = [Buf() for _ in range(2)]
                xnT = [sb(st, "cxnT%d" % i, [128, 8, WM], BF16) for i in range(2)]
                xnTb = [Buf() for _ in range(2)]
                ucv = sb(st, "ucv", [128, 8, 30 + WM], BF16)
                ucvb = [Buf() for _ in range(8)]
                ucs = sb(st, "ucs", [128, 8, 34, NSAMP], BF16)
                ucsb = [Buf() for _ in range(8)]
                u32 = sb(st, "u32", [128, 8, WM], F32)
                u32b = [Buf() for _ in range(8)]
                sig = [sb(st, "csig%d" % i, [128, WM], F32) for i in range(2)]
                sigb = [Buf() for _ in range(2)]
                v = sb(st, "cv", [128, 8, WM], F32)
                vb_ = [Buf() for _ in range(8)]
                vbf = sb(st, "cvbf", [128, 8, WM], BF16)
                vsq = sb(st, "cvsq", [128, 8, WM], BF16)
                vbfb = [Buf() for _ in range(8)]
                mean = sb(st, "cmean", [128, WM], F32)
                rstd = sb(st, "crstd", [128, WM], F32)
                msq = sb(st, "cmsq", [128, WM], F32)
                statb = Buf()
                xh = [sb(st, "cxh%d" % i, [128, WM], F32) for i in range(2)]
                xhb = [Buf() for _ in range(2)]
                zT = sb(st, "czT", [128, 8, WM], BF16)
                zTb = Buf()
                unew = sb(st, "cunew", [64, D], F32)
                unewb = Buf()
                S.op("dve", lambda e: e.memset(ucv[:, :, 0:30], 0.0), writes=ucvb)
                S.op("pool", lambda e: e.memset(zT[:], 0.0), writes=[zTb])
                stt = [xt[0][0:120, :], xt[1][0:120, :]]
                sttb = [xtb[0], xtb[1]]
                for q4 in range(4):
                    sl = q4 % 2
                    S.dma("sp", stt[sl], st_cm_conv[j][q4 * 4:(q4 + 1) * 4].rearrange("b k c -> (b k) c"),
                          writes=[sttb[sl]])
                    for ch in range(8):
                        bank = 1 + (ch % 2)
                        S.op("pe", lambda e, sl=sl, ch=ch, bank=bank: e.transpose(
                            out=ps[bank][:, 0:120], in_=stt[sl][:, ch * 128:(ch + 1) * 128], identity=ident_f[0:120, 0:120]),
                            reads=[sttb[sl], cb], writes=[pb[bank]])
                        S.op("act", lambda e, ch=ch, bank=bank, q4=q4: e.copy(
                            out=ucs[:, ch, 0:30, q4 * 4:(q4 + 1) * 4].rearrange("p k b -> p b k"),
                            in_=ps[bank][:, 0:120].rearrange("p (b k) -> p b k", k=30)),
                            reads=[pb[bank]], writes=[ucsb[ch]])
                S.dma("sp", o_s_cm_conv[j][:, 0:26, :], st_cm_conv[j][:, 4:30, :])
                groups = [list(range(g, min(g + GT, n_ptiles))) for g in range(0, n_ptiles, GT)] + [[n_ptiles]]
                obank = [0]
                for gi, tiles in enumerate(groups):
                    W = len(tiles) * 128
                    samp = (tiles[0] == n_ptiles)
                    WV = 64 if samp else W
                    xs = gi % 2
                    for jj, ti in enumerate(tiles):
                        slot = xs * GT + jj
                        S.dma("sp", xt[slot][:], xres[ti * 128:(ti + 1) * 128, :], reads=[xb[ti]], writes=[xtb[slot]])
                        nsl = c.rr % 2
                        c.rr += 1
                        norm_tile(xt[slot], xtb[slot], g_t, g_b, xn[nsl], xnb[nsl], junk, junkb, st1[nsl], st1b[nsl])
                        transpose_to(xn[nsl], xnb[nsl], 8, 0,
                                     lambda ci, jj=jj, xs=xs: xnT[xs][:, ci, jj * 128:(jj + 1) * 128], xnTb[xs])
                    for ch in range(8):
                        bank = 1 + (ch % 2)

                        def mm1(e, ch=ch, bank=bank, xs=xs, W=W):
                            ins = None
                            for half in range(2):
                                c0 = half * D + ch * 128
                                for kc in range(8):
                                    ins = e.matmul(ps[bank][:, half * 256:half * 256 + W], lhsT=pw1[:, kc, c0:c0 + 128],
                                                   rhs=xnT[xs][:, kc, 0:W], start=(kc == 0 and half == 0), stop=(kc == 7))
                            return ins
                        S.op("pe", mm1, reads=[pw1b, xnTb[xs]], writes=[pb[bank]])
                        sl = ch % 2
                        S.op("act", lambda e, ch=ch, bank=bank, sl=sl, WV=WV: e.activation(
                            out=sig[sl][:, 0:WV], in_=ps[bank][:, 256:256 + WV], func=AF.Sigmoid, bias=prm[:, 1, ch:ch + 1]),
                            reads=[pb[bank], prmb], writes=[sigb[sl]])
                        S.op("dve", lambda e, ch=ch, bank=bank, sl=sl, WV=WV: e.scalar_tensor_tensor(
                            out=u32[:, ch, 0:WV], in0=ps[bank][:, 0:WV], scalar=prm[:, 0, ch:ch + 1], in1=sig[sl][:, 0:WV],
                            op0=ALU.add, op1=ALU.mult), reads=[pb[bank], prmb, sigb[sl]], writes=[u32b[ch]])
                        if not samp:
                            S.op("act", lambda e, ch=ch, W=W: e.copy(out=ucv[:, ch, 30:30 + W], in_=u32[:, ch, 0:W]),
                                 reads=[u32b[ch]], writes=[ucvb[ch]])
                            if gi == len(groups) - 2:
                                S.dma("sp", o_p_cm_conv[j][:, ch * 128:(ch + 1) * 128].rearrange("k p -> p k"),
                                      u32[:, ch, W - 30:W], reads=[u32b[ch]], slow=True)
                        else:
                            S.op("act", lambda e, ch=ch: e.copy(out=ucs[:, ch, 30:34, :].rearrange("p s b -> p (s b)"),
                                                                in_=u32[:, ch, 0:64]),
                                 reads=[u32b[ch]], writes=[ucsb[ch]])
                    if samp:
                        for ch in range(8):
                            bank = 6 + (ch // 4)
                            S.op("pe", lambda e, ch=ch, bank=bank: e.transpose(
                                out=ps[bank][0:64, (ch % 4) * 128:(ch % 4) * 128 + 128], in_=u32[:, ch, 0:64],
                                identity=ident_f[:, :]), reads=[u32b[ch], cb], writes=[pb[bank]])
                        for hb in range(2):
                            S.op("act" if hb == 0 else "dve",
                                 (lambda e, hb=hb: e.copy(out=unew[:, hb * 512:(hb + 1) * 512], in_=ps[6 + hb][0:64, :]))
                                 if hb == 0 else
                                 (lambda e, hb=hb: e.tensor_copy(out=unew[:, hb * 512:(hb + 1) * 512], in_=ps[6 + hb][0:64, :])),
                                 reads=[pb[6 + hb]], writes=[unewb])
                        for s_ in range(4):
                            S.dma("sp", o_s_cm_conv[j][:, 26 + s_, :], unew[s_ * 16:(s_ + 1) * 16, :], reads=[unewb])
                    for ch in range(8):
                        bank = 3 + (ch % 2)

                        def mmc(e, ch=ch, bank=bank, W=W, samp=samp):
                            ins = None
                            for kk in range(31):
                                if samp:
                                    rhs = ucs[:, ch, kk:kk + 4, :].rearrange("p s b -> p (s b)")
                                    out = ps[bank][:, 0:64]
                                else:
                                    rhs = ucv[:, ch, kk:kk + W]
                                    out = ps[bank][:, 0:W]
                                ins = e.matmul(out, lhsT=dg[:, ch, kk, :], rhs=rhs, start=(kk == 0), stop=(kk == 30))
                            return ins
                        S.op("pe", mmc, reads=[dgb, ucsb[ch] if samp else ucvb[ch]], writes=[pb[bank]])
                        S.op("act", lambda e, ch=ch, bank=bank, WV=WV: e.activation(
                            out=v[:, ch, 0:WV], in_=ps[bank][:, 0:WV], func=AF.Identity, bias=prm[:, 2, ch:ch + 1]),
                            reads=[pb[bank], prmb], writes=[vb_[ch]])
                        S.op("act", lambda e, ch=ch, WV=WV: e.copy(out=vbf[:, ch, 0:WV], in_=v[:, ch, 0:WV]),
                             reads=[vb_[ch]], writes=[vbfb[ch]])
                        S.op("act", lambda e, ch=ch, WV=WV: e.activation(out=vsq[:, ch, 0:WV], in_=v[:, ch, 0:WV],
                                                                         func=AF.Square),
                             reads=[vb_[ch]], writes=[vbfb[ch]])
                    if not samp:
                        S.op("act", lambda e, W=W: e.copy(out=ucv[:, :, 0:30], in_=ucv[:, :, W:W + 30]),
                             reads=ucvb, writes=ucvb)

                    def mms(e, WV=WV):
                        ins = None
                        for ch in range(8):
                            e.matmul(ps[5][:, 0:WV], lhsT=ones_b[:, :], rhs=vbf[:, ch, 0:WV], start=(ch == 0), stop=(ch == 7))
                            ins = e.matmul(ps[5][:, 256:256 + WV], lhsT=ones_b[:, :], rhs=vsq[:, ch, 0:WV], start=False,
                                           stop=(ch == 7))
                        return ins
                    S.op("pe", mms, reads=vbfb + [prmb], writes=[pb[5]])
                    S.op("act", lambda e, WV=WV: e.activation(out=mean[:, 0:WV], in_=ps[5][:, 0:WV], func=AF.Copy,
                                                              scale=1.0 / D), reads=[pb[5]], writes=[statb])
                    S.op("dve", lambda e, WV=WV: e.tensor_tensor(out=msq[:, 0:WV], in0=mean[:, 0:WV], in1=mean[:, 0:WV],
                                                                 op=ALU.mult), reads=[statb], writes=[statb])
                    S.op("dve", lambda e, WV=WV: e.scalar_tensor_tensor(out=rstd[:, 0:WV], in0=ps[5][:, 256:256 + WV],
                                                                        scalar=1.0 / D, in1=msq[:, 0:WV], op0=ALU.mult,
                                                                        op1=ALU.subtract), reads=[pb[5], statb], writes=[statb])
                    S.op("act", lambda e, WV=WV: e.activation(out=rstd[:, 0:WV], in_=rstd[:, 0:WV], func=AF.Sqrt,
                                                              bias=eps_t[:, 0:1]), reads=[statb, cb], writes=[statb])
                    S.op("dve", lambda e, WV=WV: e.reciprocal(out=rstd[:, 0:WV], in_=rstd[:, 0:WV]), reads=[statb],
                         writes=[statb])
                    for ch in range(8):
                        sl = ch % 2
                        S.op("dve", lambda e, ch=ch, sl=sl, WV=WV: e.tensor_tensor(out=xh[sl][:, 0:WV], in0=v[:, ch, 0:WV],
                                                                                   in1=mean[:, 0:WV], op=ALU.subtract),
                             reads=[vb_[ch], statb], writes=[xhb[sl]])
                        S.op("dve", lambda e, sl=sl, WV=WV: e.tensor_tensor(out=xh[sl][:, 0:WV], in0=xh[sl][:, 0:WV],
                                                                            in1=rstd[:, 0:WV], op=ALU.mult),
                             reads=[xhb[sl], statb], writes=[xhb[sl]])
                        S.op("act", lambda e, ch=ch, sl=sl, WV=WV: e.activation(
                            out=zT[:, ch, 0:WV], in_=xh[sl][:, 0:WV], func=AF.Silu, scale=prm[:, 3, ch:ch + 1],
                            bias=prm[:, 4, ch:ch + 1]), reads=[xhb[sl], prmb], writes=[zTb])
                    for jj, ti in enumerate(tiles):
                        slot = xs * GT + jj
                        for half in range(2):
                            bank = 6 + (obank[0] % 2)
                            obank[0] += 1

                            def mm2(e, jj=jj, half=half, bank=bank):
                                ins = None
                                for ch in range(8):
                                    ins = e.matmul(ps[bank][:, :], lhsT=zT[:, ch, jj * 128:(jj + 1) * 128],
                                                   rhs=pw2[:, ch, half * 512:(half + 1) * 512],
                                                   start=(ch == 0), stop=(ch == 7))
                                return ins
                            S.op("pe", mm2, reads=[pw2b, zTb], writes=[pb[bank]])
                            S.op("dve", lambda e, slot=slot, half=half, bank=bank: e.tensor_tensor(
                                out=xt[slot][:, half * 512:(half + 1) * 512], in0=ps[bank][:, :],
                                in1=xt[slot][:, half * 512:(half + 1) * 512], op=ALU.add),
                                reads=[pb[bank], xtb[slot]], writes=[xtb[slot]])
                            S.op("dve", lambda e, slot=slot, half=half: e.tensor_tensor(
                                out=xt[slot][:, half * 512:(half + 1) * 512], in0=xt[slot][:, half * 512:(half + 1) * 512],
                                in1=b2_t[:, half * 512:(half + 1) * 512], op=ALU.add),
                                reads=[b2_b, xtb[slot]], writes=[xtb[slot]])
                        S.dma("sp", xres[ti * 128:(ti + 1) * 128, :], xt[slot][:], reads=[xtb[slot]], writes=[xb[ti]])
                S.barrier()
                S.flush()


        qkvb = [Buf("qkv%d" % i) for i in range(NT)]
        DIL = [1, 4, 16]

        def phase_qkv(li, j):
            with ExitStack() as st:
                wq, wqb = load_w_bf16(st, "wqkv", att_w_qkv[j], D, 9 * D)
                g_t, g_b = load_bcast(st, "g_mixa", norm_mix[li:li + 1, :], D)
                xt = [sb(st, "axt%d" % i, [128, D], F32) for i in range(2)]
                xtb = [Buf() for _ in range(2)]
                xn = [sb(st, "axn%d" % i, [128, D], BF16) for i in range(2)]
                xnb = [Buf() for _ in range(2)]
                junk = sb(st, "ajunk", [128, D], BF16)
                junkb = Buf()
                st1 = [sb(st, "ast%d" % i, [128, 4], F32) for i in range(2)]
                st1b = [Buf() for _ in range(2)]
                xnT = [sb(st, "axnT%d" % i, [128, 8, 128], BF16) for i in range(2)]
                xnTb = [Buf() for _ in range(2)]
                rt = [sb(st, "art%d" % i, [128, 1024], F32) for i in range(2)]
                rtb = [Buf() for _ in range(2)]
                NR = 2
                x32 = [sb(st, "ax32%d" % i, [128, 8, 128], F32) for i in range(NR)]
                x32b = [Buf() for _ in range(NR)]
                o32 = [sb(st, "ao32%d" % i, [128, 8, 128], F32) for i in range(NR)]
                o32b = [Buf() for _ in range(NR)]
                tmp = [[sb(st, "atmp%d_%d" % (i, k), [128, 8, 64], F32) for k in range(2)] for i in range(NR)]
                tmpb = [[Buf() for k in range(2)] for i in range(NR)]
                obf = [sb(st, "aobf%d" % i, [128, 1024], BF16) for i in range(NR)]
                obfb = [Buf() for _ in range(NR)]
                rr = [0]
                for ti in range(NT):
                    samp = (ti == n_ptiles)
                    sl = ti % 2
                    S.dma("sp", xt[sl][:], xres[ti * 128:(ti + 1) * 128, :], reads=[xb[ti]], writes=[xtb[sl]])
                    S.dma("sp", rt[sl][:], rope_tab[ti * 128:(ti + 1) * 128, :], writes=[rtb[sl]])
                    norm_tile(xt[sl], xtb[sl], g_t, g_b, xn[sl], xnb[sl], junk, junkb, st1[sl], st1b[sl])
                    transpose_to(xn[sl], xnb[sl], 8, 0, lambda ci, sl=sl: xnT[sl][:, ci, :], xnTb[sl])
                    cosv = rt[sl][:, 0:512].rearrange("p (h d) -> p h d", d=64)
                    sinv = rt[sl][:, 512:1024].rearrange("p (h d) -> p h d", d=64)
                    for pr in range(9):
                        g, t = pr // 3, pr % 3
                        k = rr[0] % NR
                        rr[0] += 1
                        r0 = ti * 128 - (T_P - KEEP[g])
                        need_out = (t != 0) and (samp or r0 >= 0)
                        o3 = o32[k]
                        xv = x32[k]
                        o3f = o3[:].rearrange("p h d -> p (h d)")
                        for hh in range(2):
                            cg = 2 * pr + hh
                            bank = 1 + cg % 3

                            def mm(e, cg=cg, bank=bank, sl=sl):
                                ins = None
                                for kc in range(8):
                                    ins = e.matmul(ps[bank][:, :], lhsT=xnT[sl][:, kc, :], rhs=wq[:, kc, cg * 512:(cg + 1) * 512],
                                                   start=(kc == 0), stop=(kc == 7))
                                return ins
                            S.op("pe", mm, reads=[wqb, xnTb[sl]], writes=[pb[bank]])
                            if t == 2:
                                if need_out:
                                    S.op("dve", lambda e, o3f=o3f, bank=bank, hh=hh: e.tensor_copy(
                                        out=o3f[:, hh * 512:(hh + 1) * 512], in_=ps[bank][:, :]),
                                        reads=[pb[bank]], writes=[o32b[k]])
                                else:
                                    S.op("act", lambda e, k=k, bank=bank, hh=hh: e.copy(out=obf[k][:, hh * 512:(hh + 1) * 512],
                                                                                        in_=ps[bank][:, :]),
                                         reads=[pb[bank]], writes=[obfb[k]])
                            else:
                                S.op("act", lambda e, xv=xv, bank=bank, hh=hh: e.copy(
                                    out=xv[:, hh * 4:(hh + 1) * 4, :].rearrange("p h d -> p (h d)"), in_=ps[bank][:, :]),
                                    reads=[pb[bank]], writes=[x32b[k]])
                        if t == 2:
                            if need_out:
                                S.op("act", lambda e, k=k, o3f=o3f: e.copy(out=obf[k][:], in_=o3f), reads=[o32b[k]], writes=[obfb[k]])
                        else:
                            ta, tb = tmp[k]
                            tc, td = ta, tb
                            x1 = xv[:, :, 0:64]
                            x2 = xv[:, :, 64:128]
                            S.op("dve", lambda e, ta=ta, x1=x1, cosv=cosv: e.tensor_tensor(out=ta[:], in0=x1, in1=cosv, op=ALU.mult),
                                 reads=[x32b[k], rtb[sl]], writes=[tmpb[k][0]])
                            S.op("dve", lambda e, tb=tb, x2=x2, sinv=sinv: e.tensor_tensor(out=tb[:], in0=x2, in1=sinv, op=ALU.mult),
                                 reads=[x32b[k], rtb[sl]], writes=[tmpb[k][1]])
                            S.op("dve", lambda e, ta=ta, tb=tb, o3=o3: e.tensor_tensor(out=o3[:, :, 0:64], in0=ta[:], in1=tb[:],
                                                                                     op=ALU.subtract),
                                 reads=[tmpb[k][0], tmpb[k][1]], writes=[o32b[k]])
                            S.op(ROPE_ENG, lambda e, tc=tc, x2=x2, cosv=cosv: e.tensor_tensor(out=tc[:], in0=x2, in1=cosv, op=ALU.mult),
                                 reads=[x32b[k], rtb[sl]], writes=[tmpb[k][0]])
                            S.op(ROPE_ENG, lambda e, td=td, x1=x1, sinv=sinv: e.tensor_tensor(out=td[:], in0=x1, in1=sinv, op=ALU.mult),
                                 reads=[x32b[k], rtb[sl]], writes=[tmpb[k][1]])
                            S.op(ROPE_ENG, lambda e, tc=tc, td=td, o3=o3: e.tensor_tensor(out=o3[:, :, 64:128], in0=tc[:], in1=td[:],
                                                                                        op=ALU.add),
                                 reads=[tmpb[k][0], tmpb[k][1]], writes=[o32b[k]])
                            sc = (128.0 ** -0.5) if t == 0 else 1.0
                            S.op("act", lambda e, k=k, o3f=o3f, sc=sc: e.activation(out=obf[k][:], in_=o3f, func=AF.Copy, scale=sc),
                                 reads=[o32b[k]], writes=[obfb[k]])
                        S.dma("sp", qkv_s[ti * 128:(ti + 1) * 128, pr * 1024:(pr + 1) * 1024], obf[k][:], reads=[obfb[k]],
                              writes=[qkvb[ti]])
                        if need_out:
                            if not samp:
                                dstv = o_pkv[g][0].rearrange("r t h d -> r t (h d)")
                                S.dma("sp", dstv[r0:r0 + 128, t - 1, :], o3f, reads=[o32b[k]])
                            else:
                                dstv = o_skv[g].rearrange("r t h d -> r t (h d)")
                                S.dma("sp", dstv[0:64, t - 1, :], o3f[0:64, :], reads=[o32b[k]])
                S.barrier()
                S.flush()

        ogb = [[Buf() for _ in range(n_ptiles)] for g in range(3)]

        def phase_attn_prompt():
            with ExitStack() as st:
                mask = sb(st, "amask", [128, 512], F32)
                maskb = Buf()
                S.dma("sp", mask[:], amask_d[:, :], writes=[maskb])
                blk = [sb(st, "blk%d" % i, [128, 3, 8, 128], BF16) for i in range(3)]
                blkb = [Buf() for _ in range(3)]
                vaug = [sb(st, "vaug%d" % i, [128, 8, 130], BF16) for i in range(2)]
                vaugb = [Buf() for _ in range(2)]
                kT = [sb(st, "kT%d" % i, [128, 8, 128], BF16) for i in range(2)]
                kTb = [Buf() for _ in range(2)]
                qT = [sb(st, "qT%d" % i, [128, 8, 128], BF16) for i in range(2)]
                qTb = [Buf() for _ in range(2)]
                pT = [sb(st, "pT%d" % i, [128, 512], BF16) for i in range(2)]
                pTb = [Buf() for _ in range(2)]
                pE = [sb(st, "pE%d" % i, [128, 512], F32) for i in range(2)]
                pEb = [Buf() for _ in range(2)]
                ob = [sb(st, "aob%d" % i, [128, 8, 129], F32) for i in range(2)]
                obb = [Buf() for _ in range(2)]
                for i in range(2):
                    S.op("pool", lambda e, i=i: e.memset(vaug[i][:], 1.0), writes=[vaugb[i]])
                cnt = 0
                for g in range(3):
                    dil = DIL[g]
                    nblk = (T_P // 128) // dil
                    if nblk == 0 or ("skipg%d" % g) in os.environ.get("KDBG", ""):
                        continue
                    for r in range(dil):
                        for n in range(nblk):
                            cur = cnt % 2
                            prv = 1 - cur
                            b3 = cnt % 3
                            cnt += 1
                            t0 = r + dil * 128 * n
                            tiles_touched = sorted(set((t0 + dil * jj) // 128 for jj in (0, 127)))
                            tl = list(range(tiles_touched[0], tiles_touched[-1] + 1))
                            rows = qkv_s[t0:t0 + dil * 127 + 1, g * 3072:(g + 1) * 3072]
                            if dil > 1:
                                rows = qkv_s[t0:t0 + dil * 128, g * 3072:(g + 1) * 3072].rearrange("(j q) c -> j q c", q=dil)[:, 0, :]
                            S.dma("sp", blk[b3][:].rearrange("p t h d -> p (t h d)"), rows, reads=[qkvb[x] for x in tl],
                                  writes=[blkb[b3]])
                            S.op("pool", lambda e, cur=cur, b3=b3: e.tensor_copy(out=vaug[cur][:, :, 0:128], in_=blk[b3][:, 2, :, :]),
                                 reads=[blkb[b3]], writes=[vaugb[cur]])
                            for which, dst, dstb, bank in ((0, qT[cur], qTb[cur], 0), (1, kT[cur], kTb[cur], 1)):
                                pbf = ps[bank][:].bitcast(BF16)

                                def tr(e, which=which, b3=b3, pbf=pbf):
                                    ins = None
                                    for h in range(8):
                                        ins = e.transpose(out=pbf[:, h * 128:(h + 1) * 128], in_=blk[b3][:, which, h, :],
                                                          identity=ident_b[:, :])
                                    return ins
                                S.op("pe", tr, reads=[blkb[b3], cb], writes=[pb[bank]])
                                if which == 0:
                                    S.op("act", lambda e, dst=dst, pbf=pbf: e.copy(out=dst[:].rearrange("p h q -> p (h q)"), in_=pbf),
                                         reads=[pb[bank]], writes=[dstb])
                                else:
                                    S.op("dve", lambda e, dst=dst, pbf=pbf: e.tensor_copy(out=dst[:].rearrange("p h q -> p (h q)"),
                                                                                          in_=pbf),
                                         reads=[pb[bank]], writes=[dstb])
                            for hp in range(4):
                                bank = 2 + (hp % 2)
                                pk = hp % 2

                                def mms(e, hp=hp, bank=bank, cur=cur, prv=prv, n=n):
                                    ins = None
                                    first = True
                                    for hi in range(2):
                                        h = hp * 2 + hi
                                        if n > 0:
                                            ins = e.matmul(ps[bank][:, hi * 256:hi * 256 + 128], lhsT=kT[prv][:, h, :],
                                                           rhs=qT[cur][:, h, :], start=first, stop=True)
                                            first = False
                                        ins = e.matmul(ps[bank][:, hi * 256 + 128:hi * 256 + 256], lhsT=kT[cur][:, h, :],
                                                       rhs=qT[cur][:, h, :], start=first, stop=True)
                                        first = False
                                    return ins
                                S.op("pe", mms, reads=[kTb[prv], kTb[cur], qTb[cur]], writes=[pb[bank]])
                                S.op("act", lambda e, pk=pk, bank=bank: e.activation(out=pE[pk][:], in_=ps[bank][:, :], func=AF.Exp),
                                     reads=[pb[bank]], writes=[pEb[pk]])
                                S.op("dve", lambda e, pk=pk: e.tensor_tensor(out=pT[pk][:], in0=pE[pk][:], in1=mask[:], op=ALU.mult),
                                     reads=[pEb[pk], maskb], writes=[pTb[pk]])
                                for hi in range(2):
                                    h = hp * 2 + hi
                                    obank = 4 + h // 3
                                    oc = (h % 3) * 129

                                    def mmo(e, h=h, hi=hi, pk=pk, obank=obank, oc=oc, cur=cur, prv=prv, n=n):
                                        ins = None
                                        first = (h % 3 == 0)
                                        if n > 0:
                                            ins = e.matmul(ps[obank][:, oc:oc + 129], lhsT=pT[pk][:, hi * 256:hi * 256 + 128],
                                                           rhs=vaug[prv][:, h, 0:129], start=first, stop=False)
                                            first = False
                                        ins = e.matmul(ps[obank][:, oc:oc + 129], lhsT=pT[pk][:, hi * 256 + 128:hi * 256 + 256],
                                                       rhs=vaug[cur][:, h, 0:129], start=first, stop=True)
                                        return ins
                                    S.op("pe", mmo, reads=[pTb[pk], vaugb[prv], vaugb[cur]], writes=[pb[obank]])
                            osl = cnt % 2
                            for bi, (h0, h1) in enumerate(((0, 3), (3, 6), (6, 8))):
                                nh = h1 - h0
                                if bi == 1:
                                    S.op("dve", lambda e, osl=osl, h0=h0, h1=h1, nh=nh, bi=bi: e.tensor_copy(
                                        out=ob[osl][:, h0:h1, :].rearrange("p h d -> p (h d)"), in_=ps[4 + bi][:, 0:nh * 129]),
                                        reads=[pb[4 + bi]], writes=[obb[osl]])
                                else:
                                    S.op("act", lambda e, osl=osl, h0=h0, h1=h1, nh=nh, bi=bi: e.copy(
                                        out=ob[osl][:, h0:h1, :].rearrange("p h d -> p (h d)"), in_=ps[4 + bi][:, 0:nh * 129]),
                                        reads=[pb[4 + bi]], writes=[obb[osl]])
                            if dil > 1:
                                dst = og_s[g][t0:t0 + dil * 128, :].rearrange("(j q) c -> j q c", q=dil)[:, 0, :]
                            else:
                                dst = og_s[g][t0:t0 + 128, :]
                            if "noog" not in os.environ.get("KDBG", ""):
                                S.dma("sp", dst, ob[osl][:].rearrange("p h d -> p (h d)"), reads=[obb[osl]],
                                      writes=[ogb[g][x] for x in tl])
                S.barrier()
                S.flush()


        def phase_attn_sample():
            with ExitStack() as st:
                smf = sb(st, "smf", [128, 4, 32], F32)
                smb = Buf()
                S.dma("sp", smf[:], smask_d[:, :, :], writes=[smb])
                qrow = [sb(st, "qrow%d" % i, [4, 9 * D], BF16) for i in range(2)]
                qrowb = [Buf() for _ in range(2)]
                qTs = [sb(st, "qTs%d" % i, [128, 8, 4], BF16) for i in range(2)]
                qTsb = [Buf() for _ in range(2)]
                kTn = [sb(st, "kTn%d" % i, [128, 8, 4], BF16) for i in range(2)]
                kTnb = [Buf() for _ in range(2)]
                vn = [sb(st, "vn%d" % i, [4, 8, 130], BF16) for i in range(2)]
                vnb = [Buf() for _ in range(2)]
                ct = [sb(st, "ct%d" % i, [128, 2, 8, 128], BF16) for i in range(3)]
                ctb = [Buf() for _ in range(3)]
                vas = [sb(st, "vas%d" % i, [128, 8, 130], BF16) for i in range(2)]
                vasb = [Buf() for _ in range(2)]
                kTs = [sb(st, "kTs%d" % i, [128, 8, 128], BF16) for i in range(2)]
                kTsb = [Buf() for _ in range(2)]
                pEs = [sb(st, "pEs%d" % i, [128, 32], F32) for i in range(2)]
                pEsb = [Buf() for _ in range(2)]
                pTs = [sb(st, "pTs%d" % i, [128, 8, 4], BF16) for i in range(2)]
                pTsb = [Buf() for _ in range(2)]
                osb4 = [sb(st, "osb4%d" % i, [4, 8 * 129], F32) for i in range(2)]
                osb4b = [Buf() for _ in range(2)]
                for i in range(2):
                    S.op("pool", lambda e, i=i: e.memset(vas[i][:], 1.0), writes=[vasb[i]])
                    S.op("pool", lambda e, i=i: e.memset(vn[i][:], 1.0), writes=[vnb[i]])
                osr = osamp_s.rearrange("(s b) c -> s b c", b=NSAMP)
                qsr = qkv_s[T_P:T_P + 64, :].rearrange("(s b) c -> s b c", b=NSAMP)
                tcnt = [0]
                gcnt = [0]
                for b in range(NSAMP):
                    qs = b % 2
                    S.dma("sp", qrow[qs][:], qsr[:, b, :], reads=[qkvb[n_ptiles]], writes=[qrowb[qs]])
                    started = [False, False, False]

                    def o_mm(e, lhs_fn, rhs_fn):
                        ins = None
                        for h in range(8):
                            ob_ = h // 3
                            oc = (h % 3) * 129
                            ins = e.matmul(ps[4 + ob_][0:4, oc:oc + 129], lhsT=lhs_fn(h), rhs=rhs_fn(h),
                                           start=(h % 3 == 0 and not started[ob_]), stop=False)
                        return ins
                    for g in range(3):
                        gs = gcnt[0] % 2
                        gcnt[0] += 1
                        for which, dst, dstb, bank in ((0, qTs[gs], qTsb[gs], 0), (1, kTn[gs], kTnb[gs], 1)):
                            pbf = ps[bank][:].bitcast(BF16)
                            c0 = (g * 3 + which) * D

                            def tr(e, pbf=pbf, c0=c0, qs=qs):
                                ins = None
                                for h in range(8):
                                    ins = e.transpose(out=pbf[:, h * 4:(h + 1) * 4], in_=qrow[qs][0:4, c0 + h * 128:c0 + (h + 1) * 128],
                                                      identity=ident_b[0:4, 0:4])
                                return ins
                            S.op("pe", tr, reads=[qrowb[qs], cb], writes=[pb[bank]])
                            S.op("act", lambda e, dst=dst, pbf=pbf: e.copy(out=dst[:].rearrange("p h q -> p (h q)"), in_=pbf[:, 0:32]),
                                 reads=[pb[bank]], writes=[dstb])
                        S.op("dve", lambda e, gs=gs, qs=qs, g=g: e.tensor_copy(
                            out=vn[gs][:, :, 0:128], in_=qrow[qs][0:4, (g * 3 + 2) * D:(g * 3 + 3) * D].rearrange("p (h d) -> p h d", d=128)),
                            reads=[qrowb[qs]], writes=[vnb[gs]])
                        ntile = 1 if g == 0 else 4
                        for m in range(ntile + 1):
                            new = (m == ntile)
                            ts = tcnt[0] % 2
                            c3 = tcnt[0] % 3
                            tcnt[0] += 1
                            sbank = 2 + ts
                            if not new:
                                S.dma("pool", ct[c3][:], cache_kv[g][b, m * 128:(m + 1) * 128], writes=[ctb[c3]])
                                S.op("dve", lambda e, ts=ts, c3=c3: e.tensor_copy(out=vas[ts][:, :, 0:128], in_=ct[c3][:, 1, :, :]),
                                     reads=[ctb[c3]], writes=[vasb[ts]])
                                pbf = ps[ts][:].bitcast(BF16)

                                def trk(e, pbf=pbf, c3=c3):
                                    ins = None
                                    for h in range(8):
                                        ins = e.transpose(out=pbf[:, h * 128:(h + 1) * 128], in_=ct[c3][:, 0, h, :], identity=ident_b[:, :])
                                    return ins
                                S.op("pe", trk, reads=[ctb[c3], cb], writes=[pb[ts]])
                                S.op("act", lambda e, ts=ts, pbf=pbf: e.copy(out=kTs[ts][:].rearrange("p h q -> p (h q)"), in_=pbf),
                                     reads=[pb[ts]], writes=[kTsb[ts]])
                                P_ = 128
                                kk, kkb = kTs[ts], kTsb[ts]
                                va, vab = vas[ts], vasb[ts]
                                mk = 0 if g == 0 else 1
                            else:
                                P_ = 4
                                kk, kkb = kTn[gs], kTnb[gs]
                                va, vab = vn[gs], vnb[gs]
                                mk = 2 if g == 0 else 3

                            def mms(e, sbank=sbank, kk=kk, gs=gs, P_=P_):
                                ins = None
                                for h in range(8):
                                    ins = e.matmul(ps[sbank][0:P_, h * 4:(h + 1) * 4], lhsT=kk[:, h, 0:P_], rhs=qTs[gs][:, h, :],
                                                   start=(h == 0), stop=True)
                                return ins
                            S.op("pe", mms, reads=[kkb, qTsb[gs]], writes=[pb[sbank]])
                            S.op("act", lambda e, ts=ts, sbank=sbank, P_=P_: e.activation(out=pEs[ts][0:P_, :], in_=ps[sbank][0:P_, 0:32],
                                                                                           func=AF.Exp),
                                 reads=[pb[sbank]], writes=[pEsb[ts]])
                            S.op("dve", lambda e, ts=ts, mk=mk, P_=P_: e.tensor_tensor(
                                out=pTs[ts][0:P_].rearrange("p h q -> p (h q)"), in0=pEs[ts][0:P_, :], in1=smf[0:P_, mk, :], op=ALU.mult),
                                reads=[pEsb[ts], smb], writes=[pTsb[ts]])
                            st_now = list(started)

                            def omm(e, ts=ts, va=va, P_=P_, st_now=st_now):
                                ins = None
                                for h in range(8):
                                    ob_ = h // 3
                                    oc = (h % 3) * 129
                                    ins = e.matmul(ps[4 + ob_][0:4, oc:oc + 129], lhsT=pTs[ts][0:P_, h, :], rhs=va[0:P_, h, 0:129],
                                                   start=(h % 3 == 0 and not st_now[ob_]), stop=False)
                                return ins
                            S.op("pe", omm, reads=[pTsb[ts], vab], writes=[pb[4], pb[5], pb[6]])
                            started = [True, True, True]
                    osl = b % 2
                    for bi, (h0, h1) in enumerate(((0, 3), (3, 6), (6, 8))):
                        nh = h1 - h0
                        S.op("act", lambda e, osl=osl, h0=h0, nh=nh, bi=bi: e.copy(
                            out=osb4[osl][:, h0 * 129:(h0 + nh) * 129], in_=ps[4 + bi][0:4, 0:nh * 129]),
                            reads=[pb[4 + bi]], writes=[osb4b[osl]])
                    S.dma("sp", osr[:, b, :], osb4[osl][:], reads=[osb4b[osl]], writes=[osb])
                S.barrier()
                S.flush()

        osb = Buf("osamp")

        def phase_attn_out(li, j):
            with ExitStack() as st:
                wo, wob = load_w_bf16(st, "wo", att_w_o[j], D, D)
                xt = [sb(st, "oxt%d" % i, [128, D], F32) for i in range(2)]
                xtb = [Buf() for _ in range(2)]
                acc = [[sb(st, "oacc%d_%d" % (i, g), [128, 8, 129], F32) for g in range(3)] for i in range(2)]
                accb = [[Buf() for g in range(3)] for i in range(2)]
                rden = [sb(st, "orden%d" % i, [128, 8], F32) for i in range(2)]
                rdenb = [Buf() for _ in range(2)]
                om = [sb(st, "oom%d" % i, [128, D], BF16) for i in range(2)]
                omb = [Buf() for _ in range(2)]
                omT = [sb(st, "oomT%d" % i, [128, 8, 128], BF16) for i in range(2)]
                omTb = [Buf() for _ in range(2)]
                for i in range(2):
                    S.op("pool", lambda e, i=i: e.memset(om[i][:], 0.0), writes=[omb[i]])
                for ti in range(NT):
                    samp = (ti == n_ptiles)
                    sl = ti % 2
                    P_ = 64 if samp else 128
                    S.dma("sp", xt[sl][:], xres[ti * 128:(ti + 1) * 128, :], reads=[xb[ti]], writes=[xtb[sl]])
                    a0 = acc[sl][0]
                    if not samp:
                        ng = [g for g in range(3) if (T_P // 128) // DIL[g] > 0]
                        for g in ng:
                            S.dma("sp", acc[sl][g][:].rearrange("p h d -> p (h d)"), og_s[g][ti * 128:(ti + 1) * 128, :],
                                  reads=[ogb[g][ti]], writes=[accb[sl][g]])
                        for g in ng[1:]:
                            S.op("dve", lambda e, a0=a0, sl=sl, g=g: e.tensor_tensor(out=a0[:], in0=a0[:], in1=acc[sl][g][:], op=ALU.add),
                                 reads=[accb[sl][0], accb[sl][g]], writes=[accb[sl][0]])
                    else:
                        S.dma("sp", a0[0:64].rearrange("p h d -> p (h d)"), osamp_s[:, :], reads=[osb], writes=[accb[sl][0]])
                    S.op("dve", lambda e, a0=a0, sl=sl, P_=P_: e.reciprocal(out=rden[sl][0:P_, :], in_=a0[0:P_, :, 128]),
                         reads=[accb[sl][0]], writes=[rdenb[sl]])
                    for h in range(8):
                        S.op("act", lambda e, a0=a0, sl=sl, h=h, P_=P_: e.activation(
                            out=om[sl][0:P_, h * 128:(h + 1) * 128], in_=a0[0:P_, h, 0:128], func=AF.Copy, scale=rden[sl][0:P_, h:h + 1]),
                            reads=[accb[sl][0], rdenb[sl]], writes=[omb[sl]])
                    transpose_to(om[sl], omb[sl], 8, 0, lambda ci, sl=sl: omT[sl][:, ci, :], omTb[sl])
                    for half in range(2):
                        bank = 1 + half

                        def mm2(e, half=half, bank=bank, sl=sl):
                            ins = None
                            for h in range(8):
                                ins = e.matmul(ps[bank][:, :], lhsT=omT[sl][:, h, :], rhs=wo[:, h, half * 512:(half + 1) * 512],
                                               start=(h == 0), stop=(h == 7))
                            return ins
                        S.op("pe", mm2, reads=[wob, omTb[sl]], writes=[pb[bank]])
                        S.op("dve", lambda e, sl=sl, half=half, bank=bank: e.tensor_tensor(
                            out=xt[sl][:, half * 512:(half + 1) * 512], in0=ps[bank][:, :],
                            in1=xt[sl][:, half * 512:(half + 1) * 512], op=ALU.add),
                            reads=[pb[bank], xtb[sl]], writes=[xtb[sl]])
                    S.dma("sp", xres[ti * 128:(ti + 1) * 128, :], xt[sl][:], reads=[xtb[sl]], writes=[xb[ti]])
                S.barrier()
                S.flush()

        def phase_final():
            with ExitStack() as st:
                g_t, g_b = load_bcast(st, "g_fin", norm_final[0:1, :], D)
                xt = [sb(st, "fxt%d" % i, [128, D], F32) for i in range(3)]
                xtb = [Buf() for _ in range(3)]
                st1 = [sb(st, "fst%d" % i, [128, 4], F32) for i in range(3)]
                st1b = [Buf() for _ in range(3)]
                junk = sb(st, "fjunk", [128, D], BF16)
                junkb = Buf()
                for ti in range(NT):
                    s3 = ti % 3
                    S.dma("sp", xt[s3][:], xres[ti * 128:(ti + 1) * 128, :], reads=[xb[ti]], writes=[xtb[s3]])
                    S.op("act", lambda e, s3=s3: e.activation(out=junk[:], in_=xt[s3][:], func=AF.Square,
                                                              accum_out=st1[s3][:, 0:1]),
                         reads=[xtb[s3]], writes=[junkb, st1b[s3]])
                    S.op("act", lambda e, s3=s3: e.activation(out=st1[s3][:, 1:2], in_=st1[s3][:, 0:1], func=AF.Sqrt,
                                                              scale=1.0 / D, bias=eps_t[:, 0:1]),
                         reads=[st1b[s3], cb], writes=[st1b[s3]])
                    S.op("dve", lambda e, s3=s3: e.reciprocal(out=st1[s3][:, 2:3], in_=st1[s3][:, 1:2]),
                         reads=[st1b[s3]], writes=[st1b[s3]])
                    S.op("dve", lambda e, s3=s3: e.scalar_tensor_tensor(out=xt[s3][:], in0=xt[s3][:], scalar=st1[s3][:, 2:3],
                                                                        in1=g_t[:], op0=ALU.mult, op1=ALU.mult),
                         reads=[xtb[s3], st1b[s3], g_b], writes=[xtb[s3]])
                    S.dma("sp", y_out[ti * 128:(ti + 1) * 128, :], xt[s3][:], reads=[xtb[s3]], writes=[xb[ti]])
                S.barrier()
                S.flush()

        if "lru" not in phases:
            phase_init()
        for li in range(depth):
            m, j = li % 3, li // 3
            if m == 0 and "lru" in phases:
                phase_lru(li, j)
            if m == 1 and "conv" in phases:
                phase_conv(li, j)
            if m == 2 and "attn" in phases:
                phase_qkv(li, j)
                if "nopattn" not in phases:
                    phase_attn_prompt()
                if "sattn" in phases:
                    phase_attn_sample()
                if "noaout" not in phases:
                    phase_attn_out(li, j)
            if "ffn" in phases:
                phase_ffn(li)
        phase_final()
    return nc


WEIGHT_KEYS = ("norm_mix", "norm_ffn", "ffn_w1", "ffn_w2", "lru_w_in", "lru_conv_w", "lru_conv_b", "lru_gate_a_w",
               "lru_gate_a_b", "lru_gate_x_w", "lru_gate_x_b", "lru_lambda", "lru_w_out",
               "cm_w_pw1", "cm_b_pw1", "cm_dw_w", "cm_dw_b", "cm_ln_g", "cm_ln_b", "cm_w_pw2", "cm_b_pw2",
               "att_w_qkv", "att_w_o")


def _const_tables(n_ptiles):
    T = n_ptiles * 128
    NT = n_ptiles + 1
    half = 64
    inv_freq = (np.float32(10000.0) ** (-(np.arange(half, dtype=np.float32) / np.float32(half)))).astype(np.float32)
    pos = np.zeros((NT * 128,), np.float32)
    pos[:T] = np.arange(T, dtype=np.float32)
    pos[T:T + 64] = np.float32(PAST) + np.repeat(np.arange(4, dtype=np.float32), NSAMP)
    ang = (pos[:, None] * inv_freq[None, :]).astype(np.float32)
    rope = np.concatenate([np.tile(np.cos(ang), (1, 8)), np.tile(np.sin(ang), (1, 8))], axis=1).astype(np.float32)
    k = np.arange(128)[:, None]
    q = np.arange(128)[None, :]
    one = np.concatenate([(k >= q), (k <= q)], axis=1).astype(np.float32)
    amask = np.concatenate([one, one], axis=1)
    p = np.arange(128)[:, None]
    sq = np.arange(4)[None, :]
    kinds = [(p >= sq), (p % 4 == sq), (p <= sq) & (p < 4), (p == sq)]
    smask = np.stack([np.tile(m.astype(np.float32), (1, 8)) for m in kinds], axis=1)
    return rope, amask, np.ascontiguousarray(smask)


def make_in_map(inp, core, n_ptiles=32, shared=None):
    b = core // 2
    T = n_ptiles * 128
    NT = n_ptiles + 1
    m = {}
    x = np.zeros((NT * 128, D), np.float32)
    x[:T] = inp["x_prompt"][b, :T]
    sl = slice(core * NSAMP, (core + 1) * NSAMP)
    x[T:T + 64] = np.asarray(inp["x_sample"][sl]).transpose(1, 0, 2).reshape(64, D)
    m["x_in"] = x
    if shared is None:
        shared = {}
        for k in WEIGHT_KEYS:
            shared[k] = np.ascontiguousarray(inp[k], dtype=np.float32)
        shared["norm_final"] = np.asarray(inp["norm_final"], np.float32).reshape(1, D)
        shared["ident"] = np.eye(128, dtype=np.float32)
        shared["rope_tab"], shared["amask"], shared["smask"] = _const_tables(n_ptiles)
    m.update(shared)
    m["state_lru_conv"] = np.ascontiguousarray(inp["state_lru_conv"][:, sl])
    m["state_lru_h"] = np.ascontiguousarray(inp["state_lru_h"][:, sl])
    m["state_cm_conv"] = np.ascontiguousarray(inp["state_cm_conv"][:, sl])
    m["cache_kv0"] = np.ascontiguousarray(inp["cache_kv_w128"][0, sl])
    m["cache_kv1"] = np.ascontiguousarray(inp["cache_kv_w512"][0, sl])
    c2 = np.asarray(inp["cache_kv_w2048"][0, sl])
    c2 = c2.reshape(NSAMP, 128, 16, 2, 8, 128)[:, :, 0:4].reshape(NSAMP, 512, 2, 8, 128)
    m["cache_kv2"] = np.ascontiguousarray(c2)
    return m, shared


_NC_CACHE = {}


def kernel(**inputs):
    inp = {k: np.asarray(v) for k, v in inputs.items()}
    if "nc" not in _NC_CACHE:
        _NC_CACHE["nc"] = build(n_ptiles=32, phases=("lru", "conv", "attn", "sattn", "ffn"), depth=4)
    nc = _NC_CACHE["nc"]
    in_maps = []
    shared = None
    for core in range(8):
        m, shared = make_in_map(inp, core, 32, shared)
        in_maps.append(m)
    res = run_bass_kernel_spmd(nc, in_maps, core_ids=list(range(8)))
    R = res.results
    T = SEQ
    B = 4
    y_prompt = np.stack([R[2 * b]["y_out"][:T] for b in range(B)]).astype(np.float32)
    y_sample = np.concatenate([R[c]["y_out"][T:T + 64].reshape(4, NSAMP, D).transpose(1, 0, 2) for c in range(8)], axis=0)
    def pstack(name):
        return np.stack([R[2 * b][name] for b in range(B)], axis=1)

    def scat(name):
        return np.concatenate([R[c][name] for c in range(8)], axis=1)
    outs = [y_prompt, np.ascontiguousarray(y_sample),
            pstack("p_lru_conv"), pstack("p_lru_h"), pstack("p_cm_conv"),
            np.stack([R[2 * b]["p_kv0"][0] for b in range(B)])[None],
            np.stack([R[2 * b]["p_kv1"][0] for b in range(B)])[None],
            np.stack([R[2 * b]["p_kv2"][0] for b in range(B)])[None],
            scat("s_lru_conv"), scat("s_lru_h"), scat("s_cm_conv"),
            np.concatenate([R[c]["s_kv0"].reshape(4, NSAMP, 2, 8, 128).transpose(1, 0, 2, 3, 4) for c in range(8)], axis=0)[None],
            np.concatenate([R[c]["s_kv1"].reshape(4, NSAMP, 2, 8, 128).transpose(1, 0, 2, 3, 4) for c in range(8)], axis=0)[None],
            np.concatenate([R[c]["s_kv2"].reshape(4, NSAMP, 2, 8, 128).transpose(1, 0, 2, 3, 4) for c in range(8)], axis=0)[None]]
    return tuple(np.ascontiguousarray(o, dtype=np.float32) for o in outs)
```

```python
import os
import numpy as np
from contextlib import ExitStack
import concourse.bass as bass
import concourse.mybir as mybir
from concourse.bass_utils import run_bass_kernel_spmd

F32 = mybir.dt.float32
BF16 = mybir.dt.bfloat16
AF = mybir.ActivationFunctionType
ALU = mybir.AluOpType
AX = mybir.AxisListType

D = 1024
DFF = 4096
SEQ = 4096
NSAMP = 16
DSEQ = 4
DRNN = 1408
NB = 16
BD = 88
EPS = 1e-6
PAST = 2048
SAME_ENGINE_SYNC = os.environ.get("SES", "1") == "1"
NSLOT = 8
ROPE_ENG = os.environ.get('ROPE_ENG', 'dve')


class Buf:
    __slots__ = ("name", "w", "r")

    def __init__(self, name=""):
        self.name = name
        self.w = []
        self.r = {}


class Sched:
    ENGS = ("pe", "act", "dve", "pool", "sp")

    def __init__(self, nc, es):
        self.nc = nc
        self.sems = {}
        self.val = {}
        for e in self.ENGS:
            self.sems[e] = es.enter_context(nc.semaphore("s_" + e))
            self.val[e] = 0
        for q in ("sp", "pool", "act"):
            for i in range(NSLOT):
                k = "d_%s_%d" % (q, i)
                self.sems[k] = es.enter_context(nc.semaphore(k))
                self.val[k] = 0
        self.known = {e: {} for e in self.ENGS}
        self.ops = {e: [] for e in self.ENGS}
        self.dma_k = {"sp": 0, "pool": 0, "act": 0}
        self.n_ops = 0

    def _waits(self, eng, deps):
        need = {}
        kn = self.known[eng]
        for (k, v) in deps:
            if k == eng and (eng == "pe" or not SAME_ENGINE_SYNC):
                continue
            if kn.get(k, 0) >= v:
                continue
            if need.get(k, 0) < v:
                need[k] = v
        for k, v in need.items():
            kn[k] = v
        return list(need.items())

    @staticmethod
    def _deps(reads, writes):
        deps = []
        for b in reads:
            deps += b.w
        for b in writes:
            deps += b.w
            deps += list(b.r.items())
        return deps

    @staticmethod
    def _mark(tok, reads, writes):
        for b in reads:
            if b.r.get(tok[0], 0) < tok[1]:
                b.r[tok[0]] = tok[1]
        for b in writes:
            b.w = [tok]
            b.r = {}

    def op(self, eng, fn, reads=(), writes=()):
        waits = self._waits(eng, self._deps(reads, writes))
        self.val[eng] += 1
        tok = (eng, self.val[eng])
        self.ops[eng].append((waits, fn, eng, 1))
        self._mark(tok, reads, writes)
        self.n_ops += 1
        return tok

    def dma(self, q, out, in_, reads=(), writes=(), slow=False):
        k = "d_%s_%d" % (q, self.dma_k[q] % NSLOT)
        self.dma_k[q] += 1
        deps = []
        for b in reads:
            deps += b.w
        for b in writes:
            deps += [t for t in b.w if not t[0].startswith("d_")]
            deps += list(b.r.items())
        if self.val[k] > 0:
            deps.append((k, self.val[k]))
        waits = self._waits(q, deps)
        self.val[k] += 16
        tok = (k, self.val[k])
        keep = {id(b): [t for t in b.w if t[0].startswith("d_") and t[0] != k] for b in writes}

        def fn(eng, out=out, in_=in_, slow=slow):
            if slow:
                return eng.dma_start(out=out, in_=in_, allow_slow_non_contiguous=True)
            return eng.dma_start(out=out, in_=in_)
        self.ops[q].append((waits, fn, k, 16))
        self._mark(tok, reads, writes)
        for b in writes:
            b.w = keep[id(b)] + [tok]
        self.n_ops += 1
        return tok

    def barrier(self):
        deps = [(k, v) for k, v in self.val.items() if v > 0]
        for e in self.ENGS:
            waits = self._waits(e, [d for d in deps if d[0] != e])
            if waits:
                self.ops[e].append((waits, None, None, 0))

    def flush(self):
        nc = self.nc
        amap = {"pe": "tensor", "act": "scalar", "dve": "vector", "pool": "gpsimd", "sp": "sync"}
        with nc.Block() as block:
            for e in self.ENGS:
                lst = self.ops[e]
                if not lst:
                    continue

                def body(eng, lst=lst):
                    for (waits, fn, ikey, iamt) in lst:
                        for (k, v) in waits:
                            eng.wait_ge(self.sems[k], v)
                        if fn is not None:
                            ins = fn(eng)
                            ins.then_inc(self.sems[ikey], iamt)
                getattr(block, amap[e])(body)
        self.ops = {e: [] for e in self.ENGS}


class Ctx:
    pass


LRU_DEPTH = int(os.environ.get("LRU_DEPTH", "6"))


def interleave(gens, depth):
    active = []
    it = iter(gens)
    more = True
    while True:
        while more and len(active) < depth:
            try:
                active.append(next(it))
            except StopIteration:
                more = False
        if not active:
            break
        for g in list(active):
            try:
                next(g)
            except StopIteration:
                active.remove(g)


def build(n_ptiles=32, phases=("ffn",), depth=4):
    nc = bass.Bass("TRN2", target_bir_lowering=False)
    NT = n_ptiles + 1
    T_P = n_ptiles * 128
    c = Ctx()
    c.nc = nc
    c.NT = NT

    def din(name, shape, dt=F32):
        return nc.dram_tensor(name, list(shape), dt, kind="ExternalInput").ap()

    def dout(name, shape, dt=F32):
        return nc.dram_tensor(name, list(shape), dt, kind="ExternalOutput").ap()

    def dscratch(name, shape, dt=F32):
        return nc.dram_tensor(name, list(shape), dt, kind="Internal").ap()

    x_in = din("x_in", [NT * 128, D])
    norm_mix = din("norm_mix", [4, D])
    norm_ffn = din("norm_ffn", [4, D])
    norm_final = din("norm_final", [1, D])
    ffn_w1 = din("ffn_w1", [4, D, DFF])
    ffn_w2 = din("ffn_w2", [4, DFF, D])
    ident_d = din("ident", [128, 128])
    y_out = dout("y_out", [NT * 128, D])
    lru_w_in = din("lru_w_in", [2, D, 2 * DRNN])
    lru_conv_w = din("lru_conv_w", [2, 4, DRNN])
    lru_conv_b = din("lru_conv_b", [2, DRNN])
    lru_ga_w = din("lru_gate_a_w", [2, NB, BD, BD])
    lru_ga_b = din("lru_gate_a_b", [2, DRNN])
    lru_gx_w = din("lru_gate_x_w", [2, NB, BD, BD])
    lru_gx_b = din("lru_gate_x_b", [2, DRNN])
    lru_lam = din("lru_lambda", [2, DRNN])
    lru_w_out = din("lru_w_out", [2, DRNN, D])
    att_w_qkv = din("att_w_qkv", [1, D, 9 * D])
    att_w_o = din("att_w_o", [1, D, D])
    rope_tab = din("rope_tab", [NT * 128, 512])
    amask_d = din("amask", [128, 512])
    smask_d = din("smask", [128, 4, 32])
    KEEP = [min(128, T_P), min(512, T_P), min(2048, T_P)]
    o_pkv = [dout("p_kv%d" % g, [1, KEEP[g], 2, 8, 128]) for g in range(3)]
    o_skv = [dout("s_kv%d" % g, [64, 2, 8, 128]) for g in range(3)]
    cache_kv = [din("cache_kv%d" % g, [NSAMP, [128, 512, 512][g], 2, 8, 128]) for g in range(3)]
    qkv_s = dscratch("qkv_s", [NT * 128, 9 * D], BF16)
    og_s = [dscratch("og_s%d" % g, [T_P + 16, 8 * 129]) for g in range(3)]
    osamp_s = dscratch("osamp_s", [64, 8 * 129])
    cm_w_pw1 = din("cm_w_pw1", [1, D, 2 * D])
    cm_b_pw1 = din("cm_b_pw1", [1, 2 * D])
    cm_dw_w = din("cm_dw_w", [1, 31, D])
    cm_dw_b = din("cm_dw_b", [1, D])
    cm_ln_g = din("cm_ln_g", [1, D])
    cm_ln_b = din("cm_ln_b", [1, D])
    cm_w_pw2 = din("cm_w_pw2", [1, D, D])
    cm_b_pw2 = din("cm_b_pw2", [1, D])
    st_cm_conv = din("state_cm_conv", [1, NSAMP, 30, D])
    o_p_cm_conv = dout("p_cm_conv", [1, 30, D])
    o_s_cm_conv = dout("s_cm_conv", [1, NSAMP, 30, D])
    st_lru_conv = din("state_lru_conv", [2, NSAMP, 3, DRNN])
    st_lru_h = din("state_lru_h", [2, NSAMP, DRNN])
    o_p_lru_conv = dout("p_lru_conv", [2, 3, DRNN])
    o_p_lru_h = dout("p_lru_h", [2, DRNN])
    o_s_lru_conv = dout("s_lru_conv", [2, NSAMP, 3, DRNN])
    o_s_lru_h = dout("s_lru_h", [2, NSAMP, DRNN])
    xres = dscratch("xres", [NT * 128, D])

    with ExitStack() as es:
        S = Sched(nc, es)
        c.S = S
        xb = [Buf("x%d" % i) for i in range(NT)]

        c.uid = 0

        def sb(st, name, shape, dt):
            c.uid += 1
            return st.enter_context(nc.sbuf_tensor("%s_%d" % (name, c.uid), list(shape), dt))

        ps = [es.enter_context(nc.psum_tensor("ps%d" % i, [128, 512], F32)) for i in range(8)]
        pb = [Buf("ps%d" % i) for i in range(8)]

        ident_f = sb(es, "ident_f", [128, 128], F32)
        ident_b = sb(es, "ident_b", [128, 128], BF16)
        eps_t = sb(es, "eps_t", [128, 1], F32)
        cb = Buf("consts")
        S.dma("sp", ident_f[:], ident_d[:, :], writes=[cb])
        S.op("dve", lambda e: e.tensor_copy(out=ident_b[:], in_=ident_f[:]), reads=[cb], writes=[cb])
        S.op("dve", lambda e: e.memset(eps_t[:], EPS), writes=[cb])

        def load_w_bf16(st, name, src2d, K, N, q="pool", kp=128):
            kc = K // kp
            t = sb(st, name, [kp, kc, N], BF16)
            b = Buf(name)
            src = src2d.rearrange("(kc p) n -> p kc n", p=kp)
            step = max(1, (1 << 21) // (N * 4 * kp))
            for k0 in range(0, kc, step):
                k1 = min(kc, k0 + step)
                S.dma(q, t[:, k0:k1, :], src[:, k0:k1, :], writes=[b])
            return t, b

        def load_bcast(st, name, src_row, N, q="sp"):
            t = sb(st, name, [128, N], F32)
            b = Buf(name)
            S.dma(q, t[:], src_row.partition_broadcast(128), writes=[b])
            return t, b

        c.rr = 0

        def norm_tile(xt, xtb, g_t, g_b, xn, xnb, junk, junkb, st1, st1b):
            S.op("act", lambda e: e.activation(out=junk[:], in_=xt[:], func=AF.Square, accum_out=st1[:, 0:1]),
                 reads=[xtb], writes=[junkb, st1b])
            S.op("act", lambda e: e.activation(out=st1[:, 1:2], in_=st1[:, 0:1], func=AF.Sqrt,
                                               scale=1.0 / D, bias=eps_t[:, 0:1]),
                 reads=[st1b, cb], writes=[st1b])
            S.op("dve", lambda e: e.reciprocal(out=st1[:, 2:3], in_=st1[:, 1:2]), reads=[st1b], writes=[st1b])
            S.op("dve", lambda e: e.scalar_tensor_tensor(out=xn[:], in0=xt[:], scalar=st1[:, 2:3], in1=g_t[:],
                                                         op0=ALU.mult, op1=ALU.mult),
                 reads=[xtb, st1b, g_b], writes=[xnb])

        def transpose_to(xn, xnb, nchunk, bank, dst_fn, dstb, csz=128, rows=128):
            assert nchunk == 8 and csz == 128 and rows == 128
            pbf = ps[bank][:].bitcast(BF16)

            def fn(e):
                ins = None
                for ci in range(8):
                    ins = e.transpose(out=pbf[:, ci * 128:(ci + 1) * 128], in_=xn[:, ci * 128:(ci + 1) * 128],
                                      identity=ident_b[:, :])
                return ins
            S.op("pe", fn, reads=[xnb, cb], writes=[pb[bank]])
            S.op("act", lambda e: e.copy(out=dst_fn(slice(0, 4)), in_=pbf[:, 0:512].rearrange("p (c r) -> p c r", r=128)),
                 reads=[pb[bank]], writes=[dstb])
            S.op("dve", lambda e: e.tensor_copy(out=dst_fn(slice(4, 8)),
                                                 in_=pbf[:, 512:1024].rearrange("p (c r) -> p c r", r=128)),
                 reads=[pb[bank]], writes=[dstb])

        c.sb = sb
        c.ps = ps
        c.pb = pb
        c.xb = xb

        def phase_init():
            with ExitStack() as st:
                for ti in range(NT):
                    S.dma("sp", xres[ti * 128:(ti + 1) * 128, :],
                          x_in[ti * 128:(ti + 1) * 128, :], writes=[xb[ti]])
                S.barrier()
                S.flush()

        def phase_ffn(li):
            with ExitStack() as st:
                w1, w1b = load_w_bf16(st, "w1", ffn_w1[li], D, DFF)
                w2, w2b = load_w_bf16(st, "w2", ffn_w2[li], DFF, D)
                g_t, g_b = load_bcast(st, "g_ffn", norm_ffn[li:li + 1, :], D)
                GT = 2
                xt = [sb(st, "xt%d" % i, [128, D], F32) for i in range(2 * GT)]
                xtb = [Buf() for _ in range(2 * GT)]
                xn = [sb(st, "xn%d" % i, [128, D], BF16) for i in range(2)]
                xnb = [Buf() for _ in range(2)]
                junk = sb(st, "junk", [128, D], BF16)
                junkb = Buf()
                st1 = [sb(st, "st%d" % i, [128, 4], F32) for i in range(2)]
                st1b = [Buf() for _ in range(2)]
                xnT = [sb(st, "xnT%d" % i, [128, 8, GT * 128], BF16) for i in range(2)]
                xnTb = [Buf() for _ in range(2)]
                hT = sb(st, "hT", [128, 32, GT * 128], BF16)
                hTb = Buf()
                sq = [sb(st, "sq%d" % i, [128, 2 * GT * 128], F32) for i in range(2)]
                sqb = [Buf() for _ in range(2)]
                groups = [list(range(g, min(g + GT, n_ptiles))) for g in range(0, n_ptiles, GT)] + [[n_ptiles]]
                for gi, tiles in enumerate(groups):
                    W = len(tiles) * 128
                    xs = gi % 2
                    for j, ti in enumerate(tiles):
                        slot = xs * GT + j
                        S.dma("sp", xt[slot][:], xres[ti * 128:(ti + 1) * 128, :], reads=[xb[ti]], writes=[xtb[slot]])
                        nsl = c.rr % 2
                        c.rr += 1
                        norm_tile(xt[slot], xtb[slot], g_t, g_b, xn[nsl], xnb[nsl], junk, junkb, st1[nsl], st1b[nsl])
                        transpose_to(xn[nsl], xnb[nsl], 8, 0,
                                     lambda ci, j=j, xs=xs: xnT[xs][:, ci, j * 128:(j + 1) * 128], xnTb[xs])
                    for jp in range(16):
                        bank = 1 + (jp % 3)

                        def mm1(e, jp=jp, bank=bank, xs=xs, W=W):
                            ins = None
                            for hi in range(2):
                                jc = 2 * jp + hi
                                for kc in range(8):
                                    ins = e.matmul(ps[bank][:, hi * 256:hi * 256 + W], lhsT=w1[:, kc, jc * 128:(jc + 1) * 128],
                                                   rhs=xnT[xs][:, kc, 0:W], start=(kc == 0 and hi == 0), stop=(kc == 7))
                            return ins
                        S.op("pe", mm1, reads=[w1b, xnTb[xs]], writes=[pb[bank]])
                        sl = jp % 2
                        psv = ps[bank][:, :].rearrange("p (a w) -> p a w", a=2)[:, :, 0:W]
                        sqv = sq[sl][:, :].rearrange("p (a w) -> p a w", a=2)[:, :, 0:W]
                        S.op("act", lambda e, psv=psv, sqv=sqv: e.activation(out=sqv, in_=psv, func=AF.Square),
                             reads=[pb[bank]], writes=[sqb[sl]])
                        S.op("dve", lambda e, psv=psv, sqv=sqv, jp=jp, W=W: e.scalar_tensor_tensor(
                            out=hT[:, 2 * jp:2 * jp + 2, 0:W], in0=psv, scalar=0.0, in1=sqv,
                            op0=ALU.is_gt, op1=ALU.mult), reads=[pb[bank], sqb[sl]], writes=[hTb])
                    for j, ti in enumerate(tiles):
                        slot = xs * GT + j
                        for half in range(2):
                            bank = 4 + ((2 * j + half) % 4)

                            def mm2(e, j=j, half=half, bank=bank):
                                ins = None
                                for jc in range(32):
                                    ins = e.matmul(ps[bank][:, :], lhsT=hT[:, jc, j * 128:(j + 1) * 128],
                                                   rhs=w2[:, jc, half * 512:(half + 1) * 512],
                                                   start=(jc == 0), stop=(jc == 31))
                                return ins
                            S.op("pe", mm2, reads=[w2b, hTb], writes=[pb[bank]])
                            S.op("dve", lambda e, slot=slot, half=half, bank=bank: e.tensor_tensor(
                                out=xt[slot][:, half * 512:(half + 1) * 512], in0=ps[bank][:, :],
                                in1=xt[slot][:, half * 512:(half + 1) * 512], op=ALU.add),
                                reads=[pb[bank], xtb[slot]], writes=[xtb[slot]])
                        S.dma("sp", xres[ti * 128:(ti + 1) * 128, :], xt[slot][:], reads=[xtb[slot]], writes=[xb[ti]])
                S.barrier()
                S.flush()


        def phase_lru(li, j):
            with ExitStack() as st:
                w_in, w_inb = load_w_bf16(st, "w_in", lru_w_in[j], D, 2 * DRNN)
                w_out, w_outb = load_w_bf16(st, "w_out", lru_w_out[j], DRNN, D, kp=BD)
                gaw = sb(st, "gaw", [BD, NB, BD], BF16)
                gxw = sb(st, "gxw", [BD, NB, BD], BF16)
                gwb = Buf()
                S.dma("pool", gaw[:], lru_ga_w[j].rearrange("n c d -> c n d"), writes=[gwb])
                S.dma("pool", gxw[:], lru_gx_w[j].rearrange("n c d -> c n d"), writes=[gwb])
                g_t, g_b = load_bcast(st, "g_mix", norm_mix[li:li + 1, :], D)
                prm = sb(st, "lprm", [BD, 12, NB], F32)
                prmb = Buf()
                ones = sb(st, "lones", [128, 1], F32)
                S.op("dve", lambda e: e.memset(ones[:], 1.0), writes=[prmb])
                S.dma("sp", prm[:, 0:4, :], lru_conv_w[j].rearrange("k (n p) -> p k n", p=BD), writes=[prmb], slow=True)
                for idx, src in ((4, lru_conv_b), (5, lru_ga_b), (6, lru_gx_b), (7, lru_lam)):
                    S.dma("sp", prm[:, idx, :], src[j].rearrange("(n p) -> p n", p=BD), writes=[prmb], slow=True)
                S.op("act", lambda e: e.activation(out=prm[:, 9, :], in_=prm[:, 7, :], func=AF.Exp, scale=-1.0),
                     reads=[prmb], writes=[prmb])
                S.op("act", lambda e: e.activation(out=prm[:, 8, :], in_=prm[:, 9, :], func=AF.Ln, bias=ones[0:BD, 0:1]),
                     reads=[prmb], writes=[prmb])
                S.op("dve", lambda e: e.tensor_scalar(out=prm[:, 8, :], in0=prm[:, 8, :], scalar1=-8.0, scalar2=None,
                                                      op0=ALU.mult), reads=[prmb], writes=[prmb])
                GT = 2
                WM = GT * 128
                xt = [sb(st, "lxt%d" % i, [128, D], F32) for i in range(2 * GT)]
                xtb = [Buf() for _ in range(2 * GT)]
                xn = [sb(st, "lxn%d" % i, [128, D], BF16) for i in range(2)]
                xnb = [Buf() for _ in range(2)]
                junk = sb(st, "ljunk", [128, D], BF16)
                junkb = Buf()
                st1 = [sb(st, "lst%d" % i, [128, 4], F32) for i in range(2)]
                st1b = [Buf() for _ in range(2)]
                xnT = [sb(st, "lxnT%d" % i, [128, 8, WM], BF16) for i in range(2)]
                xnTb = [Buf() for _ in range(2)]
                ubuf = sb(st, "ubuf", [BD, NB, 3 + WM], F32)
                ubufb = [Buf() for _ in range(NB)]
                ubs = sb(st, "ubs", [BD, NB, 7, NSAMP], F32)
                ubsb = [Buf() for _ in range(NB)]
                hst = sb(st, "hst", [BD, NB], F32)
                hstb = [Buf() for _ in range(NB)]
                hss = sb(st, "hss", [BD, NB, NSAMP], F32)
                hssb = [Buf() for _ in range(NB)]
                yT = sb(st, "yT", [BD, NB, WM], BF16)
                yTb = Buf()
                NW = 8
                wk = {}
                wkb = {}
                for nm, dt_ in (("gg", F32), ("gt", F32), ("A", F32), ("B", F32), ("uc", F32), ("a", F32), ("ucb", BF16)):
                    wk[nm] = [sb(st, "l" + nm + str(i), [BD, WM], dt_) for i in range(NW)]
                    wkb[nm] = [Buf() for _ in range(NW)]
                for alias, phys in (("t1", "A"), ("r", "A"), ("a2", "A"), ("h", "A"), ("t2", "B"), ("i", "B"), ("b", "B")):
                    wk[alias] = wk[phys]
                    wkb[alias] = wkb[phys]
                S.op("dve", lambda e: e.memset(ubuf[:, :, 0:3], 0.0), writes=ubufb)
                S.op("dve", lambda e: e.memset(hst[:], 0.0), writes=hstb)
                S.op("pool", lambda e: e.memset(yT[:], 0.0), writes=[yTb])
                for n in range(NB):
                    for kk in range(3):
                        S.dma("sp", ubs[:, n, kk, :], st_lru_conv[j][:, kk, n * BD:(n + 1) * BD].rearrange("b p -> p b"),
                              writes=[ubsb[n]], slow=True)
                    S.dma("sp", hss[:, n, :], st_lru_h[j][:, n * BD:(n + 1) * BD].rearrange("b p -> p b"),
                          writes=[hssb[n]], slow=True)
                groups = [list(range(g, min(g + GT, n_ptiles))) for g in range(0, n_ptiles, GT)] + [[n_ptiles]]
                obank = [0]
                for gi, tiles in enumerate(groups):
                    W = len(tiles) * 128
                    samp = (tiles[0] == n_ptiles)
                    WV = 64 if samp else W
                    xs = gi % 2
                    for jj, ti in enumerate(tiles):
                        slot = xs * GT + jj
                        S.dma("sp", xt[slot][:], xres[ti * 128:(ti + 1) * 128, :], reads=[xb[ti]], writes=[xtb[slot]])
                        nsl = c.rr % 2
                        c.rr += 1
                        norm_tile(xt[slot], xtb[slot], g_t, g_b, xn[nsl], xnb[nsl], junk, junkb, st1[nsl], st1b[nsl])
                        transpose_to(xn[nsl], xnb[nsl], 8, 0,
                                     lambda ci, jj=jj, xs=xs: xnT[xs][:, ci, jj * 128:(jj + 1) * 128], xnTb[xs])
                    def chunk(n, gi=gi, W=W, samp=samp, WV=WV, xs=xs):
                        k = n % NW
                        bank = 1 + (n % 3)
                        gbank = 4 + (n % 2)

                        def mm1(e, n=n, bank=bank, xs=xs, W=W):
                            ins = None
                            for half in range(2):
                                c0 = half * DRNN + n * BD
                                for kc in range(8):
                                    ins = e.matmul(ps[bank][0:BD, half * 256:half * 256 + W], lhsT=w_in[:, kc, c0:c0 + BD],
                                                   rhs=xnT[xs][:, kc, 0:W], start=(kc == 0 and half == 0), stop=(kc == 7))
                            return ins
                        S.op("pe", mm1, reads=[w_inb, xnTb[xs]], writes=[pb[bank]])
                        gps = ps[bank][0:BD, 0:WV]
                        ups = ps[bank][0:BD, 256:256 + WV]
                        if not samp:
                            S.op("act", lambda e, n=n, ups=ups, W=W: e.copy(out=ubuf[:, n, 3:3 + W], in_=ups),
                                 reads=[pb[bank]], writes=[ubufb[n]])
                        else:
                            S.op("act", lambda e, n=n, ups=ups: e.copy(out=ubs[:, n, 3:7, :].rearrange("p s b -> p (s b)"),
                                                                        in_=ups),
                                 reads=[pb[bank]], writes=[ubsb[n]])
                        gt = wk["gt"][k][:, 0:WV]
                        S.op("act", lambda e, gt=gt, gps=gps: e.copy(out=gt, in_=gps), reads=[pb[bank]], writes=[wkb["gt"][k]])
                        yield
                        gps = gt
                        gsrc = wkb["gt"][k]
                        t1 = wk["t1"][k][:, 0:WV]
                        t2 = wk["t2"][k][:, 0:WV]
                        gg = wk["gg"][k][:, 0:WV]
                        S.op("act", lambda e, t1=t1, gps=gps: e.activation(out=t1, in_=gps, func=AF.Square,
                                                                            scale=0.21145921592590745),
                             reads=[gsrc], writes=[wkb["t1"][k]])
                        S.op("dve", lambda e, t1=t1, t2=t2, gps=gps: e.scalar_tensor_tensor(
                            out=t2, in0=t1, scalar=1.0, in1=gps, op0=ALU.add, op1=ALU.mult),
                             reads=[wkb["t1"][k], gsrc], writes=[wkb["t2"][k]])
                        yield
                        S.op("act", lambda e, t2=t2: e.activation(out=t2, in_=t2, func=AF.Sigmoid, scale=1.5957691216057308),
                             reads=[wkb["t2"][k]], writes=[wkb["t2"][k]])
                        S.op("dve", lambda e, gg=gg, t2=t2, gps=gps: e.tensor_tensor(out=gg, in0=t2, in1=gps, op=ALU.mult),
                             reads=[wkb["t2"][k], gsrc], writes=[wkb["gg"][k]])
                        yield
                        uc = wk["uc"][k][:, 0:WV]
                        ucb = wk["ucb"][k][:, 0:WV]
                        if not samp:
                            def win(kk, n=n, W=W):
                                return ubuf[:, n, kk:kk + W]
                            ucv = uc
                            ub_ = ubufb[n]
                        else:
                            def win(kk, n=n):
                                return ubs[:, n, kk:kk + 4, :].rearrange("p s b -> p (s b)")
                            ucv = uc
                            ub_ = ubsb[n]
                        S.op("dve", lambda e, n=n, win=win, ucv=ucv: e.tensor_scalar(
                            out=ucv, in0=win(0), scalar1=prm[:, 0, n:n + 1], scalar2=prm[:, 4, n:n + 1],
                            op0=ALU.mult, op1=ALU.add), reads=[ub_, prmb], writes=[wkb["uc"][k]])
                        for kk in range(1, 4):
                            S.op("dve", lambda e, n=n, kk=kk, win=win, ucv=ucv: e.scalar_tensor_tensor(
                                out=ucv, in0=win(kk), scalar=prm[:, kk, n:n + 1], in1=ucv, op0=ALU.mult, op1=ALU.add),
                                reads=[ub_, prmb, wkb["uc"][k]], writes=[wkb["uc"][k]])
                        yield
                        S.op("act", lambda e, uc=uc, ucb=ucb: e.copy(out=ucb, in_=uc), reads=[wkb["uc"][k]],
                             writes=[wkb["ucb"][k]])
                        if not samp:
                            S.op("act", lambda e, n=n, W=W: e.copy(out=ubuf[:, n, 0:3], in_=ubuf[:, n, W:W + 3]),
                                 reads=[ubufb[n]], writes=[ubufb[n]])
                            if gi == len(groups) - 2:
                                S.dma("sp", o_p_lru_conv[j][:, n * BD:(n + 1) * BD].rearrange("k p -> p k"),
                                      ubuf[:, n, 0:3], reads=[ubufb[n]], slow=True)
                        else:
                            for kk in range(3):
                                S.dma("sp", o_s_lru_conv[j][:, kk, n * BD:(n + 1) * BD].rearrange("b p -> p b"),
                                      ubs[:, n, 4 + kk, :], reads=[ubsb[n]], slow=True)
                        yield
                        def mmg(e, n=n, gbank=gbank, ucb=ucb, WV=WV):
                            e.matmul(ps[gbank][0:BD, 0:WV], lhsT=gaw[:, n, :], rhs=ucb, start=True, stop=True)
                            return e.matmul(ps[gbank][0:BD, 256:256 + WV], lhsT=gxw[:, n, :], rhs=ucb, start=False, stop=True)
                        S.op("pe", mmg, reads=[gwb, wkb["ucb"][k]], writes=[pb[gbank]])
                        r_ = wk["r"][k][:, 0:WV]
                        i_ = wk["i"][k][:, 0:WV]
                        a_ = wk["a"][k][:, 0:WV]
                        a2 = wk["a2"][k][:, 0:WV]
                        b_ = wk["b"][k][:, 0:WV]
                        h_ = wk["h"][k][:, 0:WV]
                        S.op("act", lambda e, n=n, r_=r_, gbank=gbank, WV=WV: e.activation(
                            out=r_, in_=ps[gbank][0:BD, 0:WV], func=AF.Sigmoid, bias=prm[:, 5, n:n + 1]),
                            reads=[pb[gbank], prmb], writes=[wkb["r"][k]])
                        S.op("act", lambda e, n=n, i_=i_, gbank=gbank, WV=WV: e.activation(
                            out=i_, in_=ps[gbank][0:BD, 256:256 + WV], func=AF.Sigmoid, bias=prm[:, 6, n:n + 1]),
                            reads=[pb[gbank], prmb], writes=[wkb["i"][k]])
                        yield
                        S.op("act", lambda e, n=n, r_=r_, a_=a_: e.activation(out=a_, in_=r_, func=AF.Exp,
                                                                              scale=prm[:, 8, n:n + 1]),
                             reads=[wkb["r"][k], prmb], writes=[wkb["a"][k]])
                        yield
                        S.op("act", lambda e, a_=a_, a2=a2: e.activation(out=a2, in_=a_, func=AF.Square),
                             reads=[wkb["a"][k]], writes=[wkb["a2"][k]])
                        S.op("act", lambda e, a2=a2: e.activation(out=a2, in_=a2, func=AF.Sqrt, scale=-1.0,
                                                                  bias=ones[0:BD, 0:1]),
                             reads=[wkb["a2"][k], prmb], writes=[wkb["a2"][k]])
                        yield
                        S.op("dve", lambda e, a2=a2, i_=i_, b_=b_: e.tensor_tensor(out=b_, in0=a2, in1=i_, op=ALU.mult),
                             reads=[wkb["a2"][k], wkb["i"][k]], writes=[wkb["b"][k]])
                        S.op("dve", lambda e, b_=b_, uc=uc: e.tensor_tensor(out=b_, in0=b_, in1=uc, op=ALU.mult),
                             reads=[wkb["b"][k], wkb["uc"][k]], writes=[wkb["b"][k]])
                        if not samp:
                            S.op("dve", lambda e, n=n, a_=a_, b_=b_, h_=h_: e.tensor_tensor_scan(
                                out=h_, data0=a_, data1=b_, initial=hst[:, n:n + 1], op0=ALU.mult, op1=ALU.add),
                                reads=[wkb["a"][k], wkb["b"][k], hstb[n]], writes=[wkb["h"][k]])
                            S.op("act", lambda e, n=n, h_=h_, W=W: e.copy(out=hst[:, n:n + 1], in_=h_[:, W - 1:W]),
                                 reads=[wkb["h"][k]], writes=[hstb[n]])
                            if gi == len(groups) - 2:
                                S.dma("sp", o_p_lru_h[j:j + 1, n * BD:(n + 1) * BD].rearrange("o p -> p o"),
                                      hst[:, n:n + 1], reads=[hstb[n]], slow=True)
                        else:
                            for s_ in range(4):
                                cs_ = slice(s_ * 16, s_ * 16 + 16)
                                prev = hss[:, n, :] if s_ == 0 else h_[:, (s_ - 1) * 16:s_ * 16]
                                S.op("dve", lambda e, a_=a_, h_=h_, prev=prev, cs_=cs_: e.tensor_tensor(
                                    out=h_[:, cs_], in0=a_[:, cs_], in1=prev, op=ALU.mult),
                                    reads=[wkb["a"][k], hssb[n], wkb["h"][k]], writes=[wkb["h"][k]])
                                S.op("dve", lambda e, b_=b_, h_=h_, cs_=cs_: e.tensor_tensor(
                                    out=h_[:, cs_], in0=h_[:, cs_], in1=b_[:, cs_], op=ALU.add),
                                    reads=[wkb["b"][k], wkb["h"][k]], writes=[wkb["h"][k]])
                            S.dma("sp", o_s_lru_h[j][:, n * BD:(n + 1) * BD].rearrange("b p -> p b"),
                                  h_[:, 48:64], reads=[wkb["h"][k]], slow=True)
                        yield
                        S.op("dve", lambda e, n=n, h_=h_, gg=gg, WV=WV: e.tensor_tensor(out=yT[:, n, 0:WV], in0=h_, in1=gg,
                                                                                       op=ALU.mult),
                             reads=[wkb["h"][k], wkb["gg"][k]], writes=[yTb])
                    interleave((chunk(n) for n in range(NB)), LRU_DEPTH)
                    for jj, ti in enumerate(tiles):
                        slot = xs * GT + jj
                        for half in range(2):
                            bank = 6 + (obank[0] % 2)
                            obank[0] += 1

                            def mm2(e, jj=jj, half=half, bank=bank):
                                ins = None
                                for n in range(NB):
                                    ins = e.matmul(ps[bank][:, :], lhsT=yT[:, n, jj * 128:(jj + 1) * 128],
                                                   rhs=w_out[:, n, half * 512:(half + 1) * 512],
                                                   start=(n == 0), stop=(n == NB - 1))
                                return ins
                            S.op("pe", mm2, reads=[w_outb, yTb], writes=[pb[bank]])
                            S.op("dve", lambda e, slot=slot, half=half, bank=bank: e.tensor_tensor(
                                out=xt[slot][:, half * 512:(half + 1) * 512], in0=ps[bank][:, :],
                                in1=xt[slot][:, half * 512:(half + 1) * 512], op=ALU.add),
                                reads=[pb[bank], xtb[slot]], writes=[xtb[slot]])
                        S.dma("sp", xres[ti * 128:(ti + 1) * 128, :], xt[slot][:], reads=[xtb[slot]], writes=[xb[ti]])
                S.barrier()
                S.flush()


        def phase_conv(li, j):
            with ExitStack() as st:
                pw1, pw1b = load_w_bf16(st, "pw1", cm_w_pw1[j], D, 2 * D)
                pw2, pw2b = load_w_bf16(st, "pw2", cm_w_pw2[j], D, D)
                g_t, g_b = load_bcast(st, "g_mixc", norm_mix[li:li + 1, :], D)
                b2_t, b2_b = load_bcast(st, "b_pw2", cm_b_pw2[j:j + 1, :], D)
                prm = sb(st, "cprm", [128, 36 + 31 + 4, 8], F32)
                prmb = Buf()
                S.dma("sp", prm[:, 0, :], cm_b_pw1[j, 0:D].rearrange("(n p) -> p n", p=128), writes=[prmb], slow=True)
                S.dma("sp", prm[:, 1, :], cm_b_pw1[j, D:2 * D].rearrange("(n p) -> p n", p=128), writes=[prmb], slow=True)
                for idx, src in ((2, cm_dw_b), (3, cm_ln_g), (4, cm_ln_b)):
                    S.dma("sp", prm[:, idx, :], src[j].rearrange("(n p) -> p n", p=128), writes=[prmb], slow=True)
                for kk in range(31):
                    S.dma("sp", prm[:, 5 + kk, :], cm_dw_w[j, kk].rearrange("(n p) -> p n", p=128), writes=[prmb], slow=True)
                ones_b = sb(st, "cones", [128, 128], BF16)
                S.op("dve", lambda e: e.memset(ones_b[:], 1.0), writes=[prmb])
                dg = sb(st, "dg", [128, 8, 31, 128], BF16)
                dgb = Buf()
                for ch in range(8):
                    S.op("pool" if ch % 2 == 0 else "dve", lambda e, ch=ch: e.tensor_tensor(
                        out=dg[:, ch, :, :], in0=ident_f[:].unsqueeze(1).to_broadcast([128, 31, 128]),
                        in1=prm[:, 5:36, ch].unsqueeze(2).to_broadcast([128, 31, 128]), op=ALU.mult),
                        reads=[prmb, cb], writes=[dgb])
                GT = 2
                WM = GT * 128
                xt = [sb(st, "cxt%d" % i, [128, D], F32) for i in range(2 * GT)]
                xtb = [Buf() for _ in range(2 * GT)]
                xn = [sb(st, "cxn%d" % i, [128, D], BF16) for i in range(2)]
                xnb = [Buf() for _ in range(2)]
                junk = sb(st, "cjunk", [128, D], BF16)
                junkb = Buf()
                st1 = [sb(st, "cst%d" % i, [128, 4], F32) for i in range(2)]
                st1b = [Buf() for _ in range(2)]
                xnT = [sb(st, "cxnT%d" % i, [128, 8, WM], BF16) for i in range(2)]
                xnTb = [Buf() for _ in range(2)]
                ucv = sb(st, "ucv", [128, 8, 30 + WM], BF16)
                ucvb = [Buf() for _ in range(8)]
                ucs = sb(st, "ucs", [128, 8, 34, NSAMP], BF16)
                ucsb = [Buf() for _ in range(8)]
                u32 = sb(st, "u32", [128, 8, WM], F32)
                u32b = [Buf() for _ in range(8)]
                sig = [sb(st, "csig%d" % i, [128, WM], F32) for i in range(2)]
                sigb = [Buf() for _ in range(2)]
                v = sb(st, "cv", [128, 8, WM], F32)
                vb_ = [Buf() for _ in range(8)]
                vbf = sb(st, "cvbf", [128, 8, WM], BF16)
                vsq = sb(st, "cvsq", [128, 8, WM], BF16)
                vbfb = [Buf() for _ in range(8)]
                mean = sb(st, "cmean", [128, WM], F32)
                rstd = sb(st, "crstd", [128, WM], F32)
                msq = sb(st, "cmsq", [128, WM], F32)
                statb = Buf()
                xh = [sb(st, "cxh%d" % i, [128, WM], F32) for i in range(2)]
                xhb = [Buf() for _ in range(2)]
                zT = sb(st, "czT", [128, 8, WM], BF16)
                zTb = Buf()
                unew = sb(st, "cunew", [64, D], F32)
                unewb = Buf()
                S.op("dve", lambda e: e.memset(ucv[:, :, 0:30], 0.0), writes=ucvb)
                S.op("pool", lambda e: e.memset(zT[:], 0.0), writes=[zTb])
                stt = [xt[0][0:120, :], xt[1][0:120, :]]
                sttb = [xtb[0], xtb[1]]
                for q4 in range(4):
                    sl = q4 % 2
                    S.dma("sp", stt[sl], st_cm_conv[j][q4 * 4:(q4 + 1) * 4].rearrange("b k c -> (b k) c"),
                          writes=[sttb[sl]])
                    for ch in range(8):
                        bank = 1 + (ch % 2)
                        S.op("pe", lambda e, sl=sl, ch=ch, bank=bank: e.transpose(
                            out=ps[bank][:, 0:120], in_=stt[sl][:, ch * 128:(ch + 1) * 128], identity=ident_f[0:120, 0:120]),
                            reads=[sttb[sl], cb], writes=[pb[bank]])
                        S.op("act", lambda e, ch=ch, bank=bank, q4=q4: e.copy(
                            out=ucs[:, ch, 0:30, q4 * 4:(q4 + 1) * 4].rearrange("p k b -> p b k"),
                            in_=ps[bank][:, 0:120].rearrange("p (b k) -> p b k", k=30)),
                            reads=[pb[bank]], writes=[ucsb[ch]])
                S.dma("sp", o_s_cm_conv[j][:, 0:26, :], st_cm_conv[j][:, 4:30, :])
                groups = [list(range(g, min(g + GT, n_ptiles))) for g in range(0, n_ptiles, GT)] + [[n_ptiles]]
                obank = [0]
                for gi, tiles in enumerate(groups):
                    W = len(tiles) * 128
                    samp = (tiles[0] == n_ptiles)
                    WV = 64 if samp else W
                    xs = gi % 2
                    for jj, ti in enumerate(tiles):
                        slot = xs * GT + jj
                        S.dma("sp", xt[slot][:], xres[ti * 128:(ti + 1) * 128, :], reads=[xb[ti]], writes=[xtb[slot]])
                        nsl = c.rr % 2
                        c.rr += 1
                        norm_tile(xt[slot], xtb[slot], g_t, g_b, xn[nsl], xnb[nsl], junk, junkb, st1[nsl], st1b[nsl])
                        transpose_to(xn[nsl], xnb[nsl], 8, 0,
                                     lambda ci, jj=jj, xs=xs: xnT[xs][:, ci, jj * 128:(jj + 1) * 128], xnTb[xs])
                    for ch in range(8):
                        bank = 1 + (ch % 2)

                        def mm1(e, ch=ch, bank=bank, xs=xs, W=W):
                            ins = None
                            for half in range(2):
                                c0 = half * D + ch * 128
                                for kc in range(8):
                                    ins = e.matmul(ps[bank][:, half * 256:half * 256 + W], lhsT=pw1[:, kc, c0:c0 + 128],
                                                   rhs=xnT[xs][:, kc, 0:W], start=(kc == 0 and half == 0), stop=(kc == 7))
                            return ins
                        S.op("pe", mm1, reads=[pw1b, xnTb[xs]], writes=[pb[bank]])
                        sl = ch % 2
                        S.op("act", lambda e, ch=ch, bank=bank, sl=sl, WV=WV: e.activation(
                            out=sig[sl][:, 0:WV], in_=ps[bank][:, 256:256 + WV], func=AF.Sigmoid, bias=prm[:, 1, ch:ch + 1]),
                            reads=[pb[bank], prmb], writes=[sigb[sl]])
                        S.op("dve", lambda e, ch=ch, bank=bank, sl=sl, WV=WV: e.scalar_tensor_tensor(
                            out=u32[:, ch, 0:WV], in0=ps[bank][:, 0:WV], scalar=prm[:, 0, ch:ch + 1], in1=sig[sl][:, 0:WV],
                            op0=ALU.add, op1=ALU.mult), reads=[pb[bank], prmb, sigb[sl]], writes=[u32b[ch]])
                        if not samp:
                            S.op("act", lambda e, ch=ch, W=W: e.copy(out=ucv[:, ch, 30:30 + W], in_=u32[:, ch, 0:W]),
                                 reads=[u32b[ch]], writes=[ucvb[ch]])
                            if gi == len(groups) - 2:
                                S.dma("sp", o_p_cm_conv[j][:, ch * 128:(ch + 1) * 128].rearrange("k p -> p k"),
                                      u32[:, ch, W - 30:W], reads=[u32b[ch]], slow=True)
                        else:
                            S.op("act", lambda e, ch=ch: e.copy(out=ucs[:, ch, 30:34, :].rearrange("p s b -> p (s b)"),
                                                                in_=u32[:, ch, 0:64]),
                                 reads=[u32b[ch]], writes=[ucsb[ch]])
                    if samp:
                        for ch in range(8):
                            bank = 6 + (ch // 4)
                            S.op("pe", lambda e, ch=ch, bank=bank: e.transpose(
                                out=ps[bank][0:64, (ch % 4) * 128:(ch % 4) * 128 + 128], in_=u32[:, ch, 0:64],
                                identity=ident_f[:, :]), reads=[u32b[ch], cb], writes=[pb[bank]])
                        for hb in range(2):
                            S.op("act" if hb == 0 else "dve",
                                 (lambda e, hb=hb: e.copy(out=unew[:, hb * 512:(hb + 1) * 512], in_=ps[6 + hb][0:64, :]))
                                 if hb == 0 else
                                 (lambda e, hb=hb: e.tensor_copy(out=unew[:, hb * 512:(hb + 1) * 512], in_=ps[6 + hb][0:64, :])),
                                 reads=[pb[6 + hb]], writes=[unewb])
                        for s_ in range(4):
                            S.dma("sp", o_s_cm_conv[j][:, 26 + s_, :], unew[s_ * 16:(s_ + 1) * 16, :], reads=[unewb])
                    for ch in range(8):
                        bank = 3 + (ch % 2)

                        def mmc(e, ch=ch, bank=bank, W=W, samp=samp):
                            ins = None
                            for kk in range(31):
                                if samp:
                                    rhs = ucs[:, ch, kk:kk + 4, :].rearrange("p s b -> p (s b)")
                                    out = ps[bank][:, 0:64]
                                else:
                                    rhs = ucv[:, ch, kk:kk + W]
                                    out = ps[bank][:, 0:W]
                                ins = e.matmul(out, lhsT=dg[:, ch, kk, :], rhs=rhs, start=(kk == 0), stop=(kk == 30))
                            return ins
                        S.op("pe", mmc, reads=[dgb, ucsb[ch] if samp else ucvb[ch]], writes=[pb[bank]])
                        S.op("act", lambda e, ch=ch, bank=bank, WV=WV: e.activation(
                            out=v[:, ch, 0:WV], in_=ps[bank][:, 0:WV], func=AF.Identity, bias=prm[:, 2, ch:ch + 1]),
                            reads=[pb[bank], prmb], writes=[vb_[ch]])
                        S.op("act", lambda e, ch=ch, WV=WV: e.copy(out=vbf[:, ch, 0:WV], in_=v[:, ch, 0:WV]),
                             reads=[vb_[ch]], writes=[vbfb[ch]])
                        S.op("act", lambda e, ch=ch, WV=WV: e.activation(out=vsq[:, ch, 0:WV], in_=v[:, ch, 0:WV],
                                                                         func=AF.Square),
                             reads=[vb_[ch]], writes=[vbfb[ch]])
                    if not samp:
                        S.op("act", lambda e, W=W: e.copy(out=ucv[:, :, 0:30], in_=ucv[:, :, W:W + 30]),
                             reads=ucvb, writes=ucvb)

                    def mms(e, WV=WV):
                        ins = None
                        for ch in range(8):
                            e.matmul(ps[5][:, 0:WV], lhsT=ones_b[:, :], rhs=vbf[:, ch, 0:WV], start=(ch == 0), stop=(ch == 7))
                            ins = e.matmul(ps[5][:, 256:256 + WV], lhsT=ones_b[:, :], rhs=vsq[:, ch, 0:WV], start=False,
                                           stop=(ch == 7))
                        return ins
                    S.op("pe", mms, reads=vbfb + [prmb], writes=[pb[5]])
                    S.op("act", lambda e, WV=WV: e.activation(out=mean[:, 0:WV], in_=ps[5][:, 0:WV], func=AF.Copy,
                                                              scale=1.0 / D), reads=[pb[5]], writes=[statb])
                    S.op("dve", lambda e, WV=WV: e.tensor_tensor(out=msq[:, 0:WV], in0=mean[:, 0:WV], in1=mean[:, 0:WV],
                                                                 op=ALU.mult), reads=[statb], writes=[statb])
                    S.op("dve", lambda e, WV=WV: e.scalar_tensor_tensor(out=rstd[:, 0:WV], in0=ps[5][:, 256:256 + WV],
                                                                        scalar=1.0 / D, in1=msq[:, 0:WV], op0=ALU.mult,
                                                                        op1=ALU.subtract), reads=[pb[5], statb], writes=[statb])
                    S.op("act", lambda e, WV=WV: e.activation(out=rstd[:, 0:WV], in_=rstd[:, 0:WV], func=AF.Sqrt,
                                                              bias=eps_t[:, 0:1]), reads=[statb, cb], writes=[statb])
                    S.op("dve", lambda e, WV=WV: e.reciprocal(out=rstd[:, 0:WV], in_=rstd[:, 0:WV]), reads=[statb],
                         writes=[statb])
                    for ch in range(8):
                        sl = ch % 2
                        S.op("dve", lambda e, ch=ch, sl=sl, WV=WV: e.tensor_tensor(out=xh[sl][:, 0:WV], in0=v[:, ch, 0:WV],
                                                                                   in1=mean[:, 0:WV], op=ALU.subtract),
                             reads=[vb_[ch], statb], writes=[xhb[sl]])
                        S.op("dve", lambda e, sl=sl, WV=WV: e.tensor_tensor(out=xh[sl][:, 0:WV], in0=xh[sl][:, 0:WV],
                                                                            in1=rstd[:, 0:WV], op=ALU.mult),
                             reads=[xhb[sl], statb], writes=[xhb[sl]])
                        S.op("act", lambda e, ch=ch, sl=sl, WV=WV: e.activation(
                            out=zT[:, ch, 0:WV], in_=xh[sl][:, 0:WV], func=AF.Silu, scale=prm[:, 3, ch:ch + 1],
                            bias=prm[:, 4, ch:ch + 1]), reads=[xhb[sl], prmb], writes=[zTb])
                    for jj, ti in enumerate(tiles):
                        slot = xs * GT + jj
                        for half in range(2):
                            bank = 6 + (obank[0] % 2)
                            obank[0] += 1

                            def mm2(e, jj=jj, half=half, bank=bank):
                                ins = None
                                for ch in range(8):
                                    ins = e.matmul(ps[bank][:, :], lhsT=zT[:, ch, jj * 128:(jj + 1) * 128],
                                                   rhs=pw2[:, ch, half * 512:(half + 1) * 512],
                                                   start=(ch == 0), stop=(ch == 7))
                                return ins
                            S.op("pe", mm2, reads=[pw2b, zTb], writes=[pb[bank]])
                            S.op("dve", lambda e, slot=slot, half=half, bank=bank: e.tensor_tensor(
                                out=xt[slot][:, half * 512:(half + 1) * 512], in0=ps[bank][:, :],
                                in1=xt[slot][:, half * 512:(half + 1) * 512], op=ALU.add),
                                reads=[pb[bank], xtb[slot]], writes=[xtb[slot]])
                            S.op("dve", lambda e, slot=slot, half=half: e.tensor_tensor(
                                out=xt[slot][:, half * 512:(half + 1) * 512], in0=xt[slot][:, half * 512:(half + 1) * 512],
                                in1=b2_t[:, half * 512:(half + 1) * 512], op=ALU.add),
                                reads=[b2_b, xtb[slot]], writes=[xtb[slot]])
                        S.dma("sp", xres[ti * 128:(ti + 1) * 128, :], xt[slot][:], reads=[xtb[slot]], writes=[xb[ti]])
                S.barrier()
                S.flush()


        qkvb = [Buf("qkv%d" % i) for i in range(NT)]
        DIL = [1, 4, 16]

        def phase_qkv(li, j):
            with ExitStack() as st:
                wq, wqb = load_w_bf16(st, "wqkv", att_w_qkv[j], D, 9 * D)
                g_t, g_b = load_bcast(st, "g_mixa", norm_mix[li:li + 1, :], D)
                xt = [sb(st, "axt%d" % i, [128, D], F32) for i in range(2)]
                xtb = [Buf() for _ in range(2)]
                xn = [sb(st, "axn%d" % i, [128, D], BF16) for i in range(2)]
                xnb = [Buf() for _ in range(2)]
                junk = sb(st, "ajunk", [128, D], BF16)
                junkb = Buf()
                st1 = [sb(st, "ast%d" % i, [128, 4], F32) for i in range(2)]
                st1b = [Buf() for _ in range(2)]
                xnT = [sb(st, "axnT%d" % i, [128, 8, 128], BF16) for i in range(2)]
                xnTb = [Buf() for _ in range(2)]
                rt = [sb(st, "art%d" % i, [128, 512], F32) for i in range(2)]
                rtb = [Buf() for _ in range(2)]
                NR = 3
                x32 = [sb(st, "ax32%d" % i, [128, 4, 128], F32) for i in range(NR)]
                x32b = [Buf() for _ in range(NR)]
                o32 = [sb(st, "ao32%d" % i, [128, 4, 128], F32) for i in range(NR)]
                o32b = [Buf() for _ in range(NR)]
                tmp = [[sb(st, "atmp%d_%d" % (i, k), [128, 4, 64], F32) for k in range(4)] for i in range(NR)]
                tmpb = [[Buf() for k in range(4)] for i in range(NR)]
                obf = [sb(st, "aobf%d" % i, [128, 512], BF16) for i in range(NR)]
                obfb = [Buf() for _ in range(NR)]
                rr = [0]
                for ti in range(NT):
                    samp = (ti == n_ptiles)
                    sl = ti % 2
                    S.dma("sp", xt[sl][:], xres[ti * 128:(ti + 1) * 128, :], reads=[xb[ti]], writes=[xtb[sl]])
                    S.dma("sp", rt[sl][:], rope_tab[ti * 128:(ti + 1) * 128, :], writes=[rtb[sl]])
                    norm_tile(xt[sl], xtb[sl], g_t, g_b, xn[sl], xnb[sl], junk, junkb, st1[sl], st1b[sl])
                    transpose_to(xn[sl], xnb[sl], 8, 0, lambda ci, sl=sl: xnT[sl][:, ci, :], xnTb[sl])
                    cosv = rt[sl][:, 0:256].rearrange("p (h d) -> p h d", d=64)
                    sinv = rt[sl][:, 256:512].rearrange("p (h d) -> p h d", d=64)
                    for cg in range(18):
                        g, t, hh = cg // 6, (cg % 6) // 2, cg % 2
                        bank = 1 + cg % 3
                        k = rr[0] % NR
                        rr[0] += 1

                        def mm(e, cg=cg, bank=bank, sl=sl):
                            ins = None
                            for kc in range(8):
                                ins = e.matmul(ps[bank][:, :], lhsT=xnT[sl][:, kc, :], rhs=wq[:, kc, cg * 512:(cg + 1) * 512],
                                               start=(kc == 0), stop=(kc == 7))
                            return ins
                        S.op("pe", mm, reads=[wqb, xnTb[sl]], writes=[pb[bank]])
                        r0 = ti * 128 - (T_P - KEEP[g])
                        need_out = (t != 0) and (samp or r0 >= 0) and ("noout" not in os.environ.get("KDBG", ""))
                        if "nosampout" in os.environ.get("KDBG", "") and samp:
                            need_out = False
                        if ("nokout" in os.environ.get("KDBG", "") and t == 1) or ("novout" in os.environ.get("KDBG", "") and t == 2):
                            need_out = False
                        o3 = o32[k]
                        if t == 2:
                            if need_out:
                                S.op("dve", lambda e, o3=o3, bank=bank: e.tensor_copy(
                                    out=o3[:].rearrange("p h d -> p (h d)"), in_=ps[bank][:, :]),
                                    reads=[pb[bank]], writes=[o32b[k]])
                                S.op("act", lambda e, k=k, o3=o3: e.copy(out=obf[k][:], in_=o3[:].rearrange("p h d -> p (h d)")),
                                     reads=[o32b[k]], writes=[obfb[k]])
                            else:
                                S.op("act", lambda e, k=k, bank=bank: e.copy(out=obf[k][:], in_=ps[bank][:, :]),
                                     reads=[pb[bank]], writes=[obfb[k]])
                        else:
                            xv = x32[k]
                            S.op("act", lambda e, xv=xv, bank=bank: e.copy(out=xv[:].rearrange("p h d -> p (h d)"),
                                                                             in_=ps[bank][:, :]),
                                 reads=[pb[bank]], writes=[x32b[k]])
                            ta, tb, tc, td = tmp[k]
                            x1 = xv[:, :, 0:64]
                            x2 = xv[:, :, 64:128]
                            S.op("dve", lambda e, ta=ta, x1=x1, cosv=cosv: e.tensor_tensor(out=ta[:], in0=x1, in1=cosv, op=ALU.mult),
                                 reads=[x32b[k], rtb[sl]], writes=[tmpb[k][0]])
                            S.op("dve", lambda e, tb=tb, x2=x2, sinv=sinv: e.tensor_tensor(out=tb[:], in0=x2, in1=sinv, op=ALU.mult),
                                 reads=[x32b[k], rtb[sl]], writes=[tmpb[k][1]])
                            S.op("dve", lambda e, ta=ta, tb=tb, o3=o3: e.tensor_tensor(out=o3[:, :, 0:64], in0=ta[:], in1=tb[:],
                                                                                     op=ALU.subtract),
                                 reads=[tmpb[k][0], tmpb[k][1]], writes=[o32b[k]])
                            S.op(ROPE_ENG, lambda e, tc=tc, x2=x2, cosv=cosv: e.tensor_tensor(out=tc[:], in0=x2, in1=cosv, op=ALU.mult),
                                 reads=[x32b[k], rtb[sl]], writes=[tmpb[k][2]])
                            S.op(ROPE_ENG, lambda e, td=td, x1=x1, sinv=sinv: e.tensor_tensor(out=td[:], in0=x1, in1=sinv, op=ALU.mult),
                                 reads=[x32b[k], rtb[sl]], writes=[tmpb[k][3]])
                            S.op(ROPE_ENG, lambda e, tc=tc, td=td, o3=o3: e.tensor_tensor(out=o3[:, :, 64:128], in0=tc[:], in1=td[:],
                                                                                      op=ALU.add),
                                 reads=[tmpb[k][2], tmpb[k][3]], writes=[o32b[k]])
                            sc = (128.0 ** -0.5) if t == 0 else 1.0
                            S.op("act", lambda e, k=k, o3=o3, sc=sc: e.activation(
                                out=obf[k][:], in_=o3[:].rearrange("p h d -> p (h d)"), func=AF.Copy, scale=sc),
                                reads=[o32b[k]], writes=[obfb[k]])
                        S.dma("sp", qkv_s[ti * 128:(ti + 1) * 128, cg * 512:(cg + 1) * 512], obf[k][:], reads=[obfb[k]],
                              writes=[qkvb[ti]])
                        if need_out:
                            o3f = o3[:].rearrange("p h d -> p (h d)")
                            if not samp:
                                dstv = o_pkv[g][0].rearrange("r t h d -> r t (h d)")
                                S.dma("sp", dstv[r0:r0 + 128, t - 1, hh * 512:(hh + 1) * 512], o3f, reads=[o32b[k]])
                            else:
                                dstv = o_skv[g].rearrange("r t h d -> r t (h d)")
                                S.dma("sp", dstv[0:64, t - 1, hh * 512:(hh + 1) * 512], o3f[0:64, :], reads=[o32b[k]])
                S.barrier()
                S.flush()

        ogb = [[Buf() for _ in range(n_ptiles)] for g in range(3)]

        def phase_attn_prompt():
            with ExitStack() as st:
                mask = sb(st, "amask", [128, 512], F32)
                maskb = Buf()
                S.dma("sp", mask[:], amask_d[:, :], writes=[maskb])
                blk = [sb(st, "blk%d" % i, [128, 3, 8, 128], BF16) for i in range(3)]
                blkb = [Buf() for _ in range(3)]
                vaug = [sb(st, "vaug%d" % i, [128, 8, 130], BF16) for i in range(2)]
                vaugb = [Buf() for _ in range(2)]
                kT = [sb(st, "kT%d" % i, [128, 8, 128], BF16) for i in range(2)]
                kTb = [Buf() for _ in range(2)]
                qT = [sb(st, "qT%d" % i, [128, 8, 128], BF16) for i in range(2)]
                qTb = [Buf() for _ in range(2)]
                pT = [sb(st, "pT%d" % i, [128, 512], BF16) for i in range(2)]
                pTb = [Buf() for _ in range(2)]
                pE = [sb(st, "pE%d" % i, [128, 512], F32) for i in range(2)]
                pEb = [Buf() for _ in range(2)]
                ob = [sb(st, "aob%d" % i, [128, 8, 129], F32) for i in range(2)]
                obb = [Buf() for _ in range(2)]
                for i in range(2):
                    S.op("pool", lambda e, i=i: e.memset(vaug[i][:], 1.0), writes=[vaugb[i]])
                cnt = 0
                for g in range(3):
                    dil = DIL[g]
                    nblk = (T_P // 128) // dil
                    if nblk == 0 or ("skipg%d" % g) in os.environ.get("KDBG", ""):
                        continue
                    for r in range(dil):
                        for n in range(nblk):
                            cur = cnt % 2
                            prv = 1 - cur
                            b3 = cnt % 3
                            cnt += 1
                            t0 = r + dil * 128 * n
                            tiles_touched = sorted(set((t0 + dil * jj) // 128 for jj in (0, 127)))
                            tl = list(range(tiles_touched[0], tiles_touched[-1] + 1))
                            rows = qkv_s[t0:t0 + dil * 127 + 1, g * 3072:(g + 1) * 3072]
                            if dil > 1:
                                rows = qkv_s[t0:t0 + dil * 128, g * 3072:(g + 1) * 3072].rearrange("(j q) c -> j q c", q=dil)[:, 0, :]
                            S.dma("sp", blk[b3][:].rearrange("p t h d -> p (t h d)"), rows, reads=[qkvb[x] for x in tl],
                                  writes=[blkb[b3]])
                            S.op("pool", lambda e, cur=cur, b3=b3: e.tensor_copy(out=vaug[cur][:, :, 0:128], in_=blk[b3][:, 2, :, :]),
                                 reads=[blkb[b3]], writes=[vaugb[cur]])
                            for which, dst, dstb, bank in ((0, qT[cur], qTb[cur], 0), (1, kT[cur], kTb[cur], 1)):
                                pbf = ps[bank][:].bitcast(BF16)

                                def tr(e, which=which, b3=b3, pbf=pbf):
                                    ins = None
                                    for h in range(8):
                                        ins = e.transpose(out=pbf[:, h * 128:(h + 1) * 128], in_=blk[b3][:, which, h, :],
                                                          identity=ident_b[:, :])
                                    return ins
                                S.op("pe", tr, reads=[blkb[b3], cb], writes=[pb[bank]])
                                if which == 0:
                                    S.op("act", lambda e, dst=dst, pbf=pbf: e.copy(out=dst[:].rearrange("p h q -> p (h q)"), in_=pbf),
                                         reads=[pb[bank]], writes=[dstb])
                                else:
                                    S.op("dve", lambda e, dst=dst, pbf=pbf: e.tensor_copy(out=dst[:].rearrange("p h q -> p (h q)"),
                                                                                          in_=pbf),
                                         reads=[pb[bank]], writes=[dstb])
                            for hp in range(4):
                                bank = 2 + (hp % 2)
                                pk = hp % 2

                                def mms(e, hp=hp, bank=bank, cur=cur, prv=prv, n=n):
                                    ins = None
                                    first = True
                                    for hi in range(2):
                                        h = hp * 2 + hi
                                        if n > 0:
                                            ins = e.matmul(ps[bank][:, hi * 256:hi * 256 + 128], lhsT=kT[prv][:, h, :],
                                                           rhs=qT[cur][:, h, :], start=first, stop=True)
                                            first = False
                                        ins = e.matmul(ps[bank][:, hi * 256 + 128:hi * 256 + 256], lhsT=kT[cur][:, h, :],
                                                       rhs=qT[cur][:, h, :], start=first, stop=True)
                                        first = False
                                    return ins
                                S.op("pe", mms, reads=[kTb[prv], kTb[cur], qTb[cur]], writes=[pb[bank]])
                                S.op("act", lambda e, pk=pk, bank=bank: e.activation(out=pE[pk][:], in_=ps[bank][:, :], func=AF.Exp),
                                     reads=[pb[bank]], writes=[pEb[pk]])
                                S.op("dve", lambda e, pk=pk: e.tensor_tensor(out=pT[pk][:], in0=pE[pk][:], in1=mask[:], op=ALU.mult),
                                     reads=[pEb[pk], maskb], writes=[pTb[pk]])
                                for hi in range(2):
                                    h = hp * 2 + hi
                                    obank = 4 + h // 3
                                    oc = (h % 3) * 129

                                    def mmo(e, h=h, hi=hi, pk=pk, obank=obank, oc=oc, cur=cur, prv=prv, n=n):
                                        ins = None
                                        first = (h % 3 == 0)
                                        if n > 0:
                                            ins = e.matmul(ps[obank][:, oc:oc + 129], lhsT=pT[pk][:, hi * 256:hi * 256 + 128],
                                                           rhs=vaug[prv][:, h, 0:129], start=first, stop=False)
                                            first = False
                                        ins = e.matmul(ps[obank][:, oc:oc + 129], lhsT=pT[pk][:, hi * 256 + 128:hi * 256 + 256],
                                                       rhs=vaug[cur][:, h, 0:129], start=first, stop=True)
                                        return ins
                                    S.op("pe", mmo, reads=[pTb[pk], vaugb[prv], vaugb[cur]], writes=[pb[obank]])
                            osl = cnt % 2
                            for bi, (h0, h1) in enumerate(((0, 3), (3, 6), (6, 8))):
                                nh = h1 - h0
                                if bi == 1:
                                    S.op("dve", lambda e, osl=osl, h0=h0, h1=h1, nh=nh, bi=bi: e.tensor_copy(
                                        out=ob[osl][:, h0:h1, :].rearrange("p h d -> p (h d)"), in_=ps[4 + bi][:, 0:nh * 129]),
                                        reads=[pb[4 + bi]], writes=[obb[osl]])
                                else:
                                    S.op("act", lambda e, osl=osl, h0=h0, h1=h1, nh=nh, bi=bi: e.copy(
                                        out=ob[osl][:, h0:h1, :].rearrange("p h d -> p (h d)"), in_=ps[4 + bi][:, 0:nh * 129]),
                                        reads=[pb[4 + bi]], writes=[obb[osl]])
                            if dil > 1:
                                dst = og_s[g][t0:t0 + dil * 128, :].rearrange("(j q) c -> j q c", q=dil)[:, 0, :]
                            else:
                                dst = og_s[g][t0:t0 + 128, :]
                            if "noog" not in os.environ.get("KDBG", ""):
                                S.dma("sp", dst, ob[osl][:].rearrange("p h d -> p (h d)"), reads=[obb[osl]],
                                      writes=[ogb[g][x] for x in tl])
                S.barrier()
                S.flush()


        def phase_attn_sample():
            with ExitStack() as st:
                smf = sb(st, "smf", [128, 4, 32], F32)
                smb = Buf()
                S.dma("sp", smf[:], smask_d[:, :, :], writes=[smb])
                qrow = [sb(st, "qrow%d" % i, [4, 9 * D], BF16) for i in range(2)]
                qrowb = [Buf() for _ in range(2)]
                qTs = [sb(st, "qTs%d" % i, [128, 8, 4], BF16) for i in range(2)]
                qTsb = [Buf() for _ in range(2)]
                kTn = [sb(st, "kTn%d" % i, [128, 8, 4], BF16) for i in range(2)]
                kTnb = [Buf() for _ in range(2)]
                vn = [sb(st, "vn%d" % i, [4, 8, 130], BF16) for i in range(2)]
                vnb = [Buf() for _ in range(2)]
                ct = [sb(st, "ct%d" % i, [128, 2, 8, 128], BF16) for i in range(3)]
                ctb = [Buf() for _ in range(3)]
                vas = [sb(st, "vas%d" % i, [128, 8, 130], BF16) for i in range(2)]
                vasb = [Buf() for _ in range(2)]
                kTs = [sb(st, "kTs%d" % i, [128, 8, 128], BF16) for i in range(2)]
                kTsb = [Buf() for _ in range(2)]
                pEs = [sb(st, "pEs%d" % i, [128, 32], F32) for i in range(2)]
                pEsb = [Buf() for _ in range(2)]
                pTs = [sb(st, "pTs%d" % i, [128, 8, 4], BF16) for i in range(2)]
                pTsb = [Buf() for _ in range(2)]
                osb4 = [sb(st, "osb4%d" % i, [4, 8 * 129], F32) for i in range(2)]
                osb4b = [Buf() for _ in range(2)]
                for i in range(2):
                    S.op("pool", lambda e, i=i: e.memset(vas[i][:], 1.0), writes=[vasb[i]])
                    S.op("pool", lambda e, i=i: e.memset(vn[i][:], 1.0), writes=[vnb[i]])
                osr = osamp_s.rearrange("(s b) c -> s b c", b=NSAMP)
                qsr = qkv_s[T_P:T_P + 64, :].rearrange("(s b) c -> s b c", b=NSAMP)
                tcnt = [0]
                gcnt = [0]
                for b in range(NSAMP):
                    qs = b % 2
                    S.dma("sp", qrow[qs][:], qsr[:, b, :], reads=[qkvb[n_ptiles]], writes=[qrowb[qs]])
                    started = [False, False, False]

                    def o_mm(e, lhs_fn, rhs_fn):
                        ins = None
                        for h in range(8):
                            ob_ = h // 3
                            oc = (h % 3) * 129
                            ins = e.matmul(ps[4 + ob_][0:4, oc:oc + 129], lhsT=lhs_fn(h), rhs=rhs_fn(h),
                                           start=(h % 3 == 0 and not started[ob_]), stop=False)
                        return ins
                    for g in range(3):
                        gs = gcnt[0] % 2
                        gcnt[0] += 1
                        for which, dst, dstb, bank in ((0, qTs[gs], qTsb[gs], 0), (1, kTn[gs], kTnb[gs], 1)):
                            pbf = ps[bank][:].bitcast(BF16)
                            c0 = (g * 3 + which) * D

                            def tr(e, pbf=pbf, c0=c0, qs=qs):
                                ins = None
                                for h in range(8):
                                    ins = e.transpose(out=pbf[:, h * 4:(h + 1) * 4], in_=qrow[qs][0:4, c0 + h * 128:c0 + (h + 1) * 128],
                                                      identity=ident_b[0:4, 0:4])
                                return ins
                            S.op("pe", tr, reads=[qrowb[qs], cb], writes=[pb[bank]])
                            S.op("act", lambda e, dst=dst, pbf=pbf: e.copy(out=dst[:].rearrange("p h q -> p (h q)"), in_=pbf[:, 0:32]),
                                 reads=[pb[bank]], writes=[dstb])
                        S.op("dve", lambda e, gs=gs, qs=qs, g=g: e.tensor_copy(
                            out=vn[gs][:, :, 0:128], in_=qrow[qs][0:4, (g * 3 + 2) * D:(g * 3 + 3) * D].rearrange("p (h d) -> p h d", d=128)),
                            reads=[qrowb[qs]], writes=[vnb[gs]])
                        ntile = 1 if g == 0 else 4
                        for m in range(ntile + 1):
                            new = (m == ntile)
                            ts = tcnt[0] % 2
                            c3 = tcnt[0] % 3
                            tcnt[0] += 1
                            sbank = 2 + ts
                            if not new:
                                S.dma("pool", ct[c3][:], cache_kv[g][b, m * 128:(m + 1) * 128], writes=[ctb[c3]])
                                S.op("dve", lambda e, ts=ts, c3=c3: e.tensor_copy(out=vas[ts][:, :, 0:128], in_=ct[c3][:, 1, :, :]),
                                     reads=[ctb[c3]], writes=[vasb[ts]])
                                pbf = ps[ts][:].bitcast(BF16)

                                def trk(e, pbf=pbf, c3=c3):
                                    ins = None
                                    for h in range(8):
                                        ins = e.transpose(out=pbf[:, h * 128:(h + 1) * 128], in_=ct[c3][:, 0, h, :], identity=ident_b[:, :])
                                    return ins
                                S.op("pe", trk, reads=[ctb[c3], cb], writes=[pb[ts]])
                                S.op("act", lambda e, ts=ts, pbf=pbf: e.copy(out=kTs[ts][:].rearrange("p h q -> p (h q)"), in_=pbf),
                                     reads=[pb[ts]], writes=[kTsb[ts]])
                                P_ = 128
                                kk, kkb = kTs[ts], kTsb[ts]
                                va, vab = vas[ts], vasb[ts]
                                mk = 0 if g == 0 else 1
                            else:
                                P_ = 4
                                kk, kkb = kTn[gs], kTnb[gs]
                                va, vab = vn[gs], vnb[gs]
                                mk = 2 if g == 0 else 3

                            def mms(e, sbank=sbank, kk=kk, gs=gs, P_=P_):
                                ins = None
                                for h in range(8):
                                    ins = e.matmul(ps[sbank][0:P_, h * 4:(h + 1) * 4], lhsT=kk[:, h, 0:P_], rhs=qTs[gs][:, h, :],
                                                   start=(h == 0), stop=True)
                                return ins
                            S.op("pe", mms, reads=[kkb, qTsb[gs]], writes=[pb[sbank]])
                            S.op("act", lambda e, ts=ts, sbank=sbank, P_=P_: e.activation(out=pEs[ts][0:P_, :], in_=ps[sbank][0:P_, 0:32],
                                                                                           func=AF.Exp),
                                 reads=[pb[sbank]], writes=[pEsb[ts]])
                            S.op("dve", lambda e, ts=ts, mk=mk, P_=P_: e.tensor_tensor(
                                out=pTs[ts][0:P_].rearrange("p h q -> p (h q)"), in0=pEs[ts][0:P_, :], in1=smf[0:P_, mk, :], op=ALU.mult),
                                reads=[pEsb[ts], smb], writes=[pTsb[ts]])
                            st_now = list(started)

                            def omm(e, ts=ts, va=va, P_=P_, st_now=st_now):
                                ins = None
                                for h in range(8):
                                    ob_ = h // 3
                                    oc = (h % 3) * 129
                                    ins = e.matmul(ps[4 + ob_][0:4, oc:oc + 129], lhsT=pTs[ts][0:P_, h, :], rhs=va[0:P_, h, 0:129],
                                                   start=(h % 3 == 0 and not st_now[ob_]), stop=False)
                                return ins
                            S.op("pe", omm, reads=[pTsb[ts], vab], writes=[pb[4], pb[5], pb[6]])
                            started = [True, True, True]
                    osl = b % 2
                    for bi, (h0, h1) in enumerate(((0, 3), (3, 6), (6, 8))):
                        nh = h1 - h0
                        S.op("act", lambda e, osl=osl, h0=h0, nh=nh, bi=bi: e.copy(
                            out=osb4[osl][:, h0 * 129:(h0 + nh) * 129], in_=ps[4 + bi][0:4, 0:nh * 129]),
                            reads=[pb[4 + bi]], writes=[osb4b[osl]])
                    S.dma("sp", osr[:, b, :], osb4[osl][:], reads=[osb4b[osl]], writes=[osb])
                S.barrier()
                S.flush()

        osb = Buf("osamp")

        def phase_attn_out(li, j):
            with ExitStack() as st:
                wo, wob = load_w_bf16(st, "wo", att_w_o[j], D, D)
                xt = [sb(st, "oxt%d" % i, [128, D], F32) for i in range(2)]
                xtb = [Buf() for _ in range(2)]
                acc = [[sb(st, "oacc%d_%d" % (i, g), [128, 8, 129], F32) for g in range(3)] for i in range(2)]
                accb = [[Buf() for g in range(3)] for i in range(2)]
                rden = [sb(st, "orden%d" % i, [128, 8], F32) for i in range(2)]
                rdenb = [Buf() for _ in range(2)]
                om = [sb(st, "oom%d" % i, [128, D], BF16) for i in range(2)]
                omb = [Buf() for _ in range(2)]
                omT = [sb(st, "oomT%d" % i, [128, 8, 128], BF16) for i in range(2)]
                omTb = [Buf() for _ in range(2)]
                for i in range(2):
                    S.op("pool", lambda e, i=i: e.memset(om[i][:], 0.0), writes=[omb[i]])
                for ti in range(NT):
                    samp = (ti == n_ptiles)
                    sl = ti % 2
                    P_ = 64 if samp else 128
                    S.dma("sp", xt[sl][:], xres[ti * 128:(ti + 1) * 128, :], reads=[xb[ti]], writes=[xtb[sl]])
                    a0 = acc[sl][0]
                    if not samp:
                        ng = [g for g in range(3) if (T_P // 128) // DIL[g] > 0]
                        for g in ng:
                            S.dma("sp", acc[sl][g][:].rearrange("p h d -> p (h d)"), og_s[g][ti * 128:(ti + 1) * 128, :],
                                  reads=[ogb[g][ti]], writes=[accb[sl][g]])
                        for g in ng[1:]:
                            S.op("dve", lambda e, a0=a0, sl=sl, g=g: e.tensor_tensor(out=a0[:], in0=a0[:], in1=acc[sl][g][:], op=ALU.add),
                                 reads=[accb[sl][0], accb[sl][g]], writes=[accb[sl][0]])
                    else:
                        S.dma("sp", a0[0:64].rearrange("p h d -> p (h d)"), osamp_s[:, :], reads=[osb], writes=[accb[sl][0]])
                    S.op("dve", lambda e, a0=a0, sl=sl, P_=P_: e.reciprocal(out=rden[sl][0:P_, :], in_=a0[0:P_, :, 128]),
                         reads=[accb[sl][0]], writes=[rdenb[sl]])
                    for h in range(8):
                        S.op("act", lambda e, a0=a0, sl=sl, h=h, P_=P_: e.activation(
                            out=om[sl][0:P_, h * 128:(h + 1) * 128], in_=a0[0:P_, h, 0:128], func=AF.Copy, scale=rden[sl][0:P_, h:h + 1]),
                            reads=[accb[sl][0], rdenb[sl]], writes=[omb[sl]])
                    transpose_to(om[sl], omb[sl], 8, 0, lambda ci, sl=sl: omT[sl][:, ci, :], omTb[sl])
                    for half in range(2):
                        bank = 1 + half

                        def mm2(e, half=half, bank=bank, sl=sl):
                            ins = None
                            for h in range(8):
                                ins = e.matmul(ps[bank][:, :], lhsT=omT[sl][:, h, :], rhs=wo[:, h, half * 512:(half + 1) * 512],
                                               start=(h == 0), stop=(h == 7))
                            return ins
                        S.op("pe", mm2, reads=[wob, omTb[sl]], writes=[pb[bank]])
                        S.op("dve", lambda e, sl=sl, half=half, bank=bank: e.tensor_tensor(
                            out=xt[sl][:, half * 512:(half + 1) * 512], in0=ps[bank][:, :],
                            in1=xt[sl][:, half * 512:(half + 1) * 512], op=ALU.add),
                            reads=[pb[bank], xtb[sl]], writes=[xtb[sl]])
                    S.dma("sp", xres[ti * 128:(ti + 1) * 128, :], xt[sl][:], reads=[xtb[sl]], writes=[xb[ti]])
                S.barrier()
                S.flush()

        def phase_final():
            with ExitStack() as st:
                g_t, g_b = load_bcast(st, "g_fin", norm_final[0:1, :], D)
                xt = [sb(st, "fxt%d" % i, [128, D], F32) for i in range(3)]
                xtb = [Buf() for _ in range(3)]
                st1 = [sb(st, "fst%d" % i, [128, 4], F32) for i in range(3)]
                st1b = [Buf() for _ in range(3)]
                junk = sb(st, "fjunk", [128, D], BF16)
                junkb = Buf()
                for ti in range(NT):
                    s3 = ti % 3
                    S.dma("sp", xt[s3][:], xres[ti * 128:(ti + 1) * 128, :], reads=[xb[ti]], writes=[xtb[s3]])
                    S.op("act", lambda e, s3=s3: e.activation(out=junk[:], in_=xt[s3][:], func=AF.Square,
                                                              accum_out=st1[s3][:, 0:1]),
                         reads=[xtb[s3]], writes=[junkb, st1b[s3]])
                    S.op("act", lambda e, s3=s3: e.activation(out=st1[s3][:, 1:2], in_=st1[s3][:, 0:1], func=AF.Sqrt,
                                                              scale=1.0 / D, bias=eps_t[:, 0:1]),
                         reads=[st1b[s3], cb], writes=[st1b[s3]])
                    S.op("dve", lambda e, s3=s3: e.reciprocal(out=st1[s3][:, 2:3], in_=st1[s3][:, 1:2]),
                         reads=[st1b[s3]], writes=[st1b[s3]])
                    S.op("dve", lambda e, s3=s3: e.scalar_tensor_tensor(out=xt[s3][:], in0=xt[s3][:], scalar=st1[s3][:, 2:3],
                                                                        in1=g_t[:], op0=ALU.mult, op1=ALU.mult),
                         reads=[xtb[s3], st1b[s3], g_b], writes=[xtb[s3]])
                    S.dma("sp", y_out[ti * 128:(ti + 1) * 128, :], xt[s3][:], reads=[xtb[s3]], writes=[xb[ti]])
                S.barrier()
                S.flush()

        phase_init()
        for li in range(depth):
            m, j = li % 3, li // 3
            if m == 0 and "lru" in phases:
                phase_lru(li, j)
            if m == 1 and "conv" in phases:
                phase_conv(li, j)
            if m == 2 and "attn" in phases:
                phase_qkv(li, j)
                if "nopattn" not in phases:
                    phase_attn_prompt()
                if "sattn" in phases:
                    phase_attn_sample()
                if "noaout" not in phases:
                    phase_attn_out(li, j)
            if "ffn" in phases:
                phase_ffn(li)
        phase_final()
    return nc


WEIGHT_KEYS = ("norm_mix", "norm_ffn", "ffn_w1", "ffn_w2", "lru_w_in", "lru_conv_w", "lru_conv_b", "lru_gate_a_w",
               "lru_gate_a_b", "lru_gate_x_w", "lru_gate_x_b", "lru_lambda", "lru_w_out",
               "cm_w_pw1", "cm_b_pw1", "cm_dw_w", "cm_dw_b", "cm_ln_g", "cm_ln_b", "cm_w_pw2", "cm_b_pw2",
               "att_w_qkv", "att_w_o")


def _const_tables(n_ptiles):
    T = n_ptiles * 128
    NT = n_ptiles + 1
    half = 64
    inv_freq = (np.float32(10000.0) ** (-(np.arange(half, dtype=np.float32) / np.float32(half)))).astype(np.float32)
    pos = np.zeros((NT * 128,), np.float32)
    pos[:T] = np.arange(T, dtype=np.float32)
    pos[T:T + 64] = np.float32(PAST) + np.repeat(np.arange(4, dtype=np.float32), NSAMP)
    ang = (pos[:, None] * inv_freq[None, :]).astype(np.float32)
    rope = np.concatenate([np.tile(np.cos(ang), (1, 4)), np.tile(np.sin(ang), (1, 4))], axis=1).astype(np.float32)
    k = np.arange(128)[:, None]
    q = np.arange(128)[None, :]
    one = np.concatenate([(k >= q), (k <= q)], axis=1).astype(np.float32)
    amask = np.concatenate([one, one], axis=1)
    p = np.arange(128)[:, None]
    sq = np.arange(4)[None, :]
    kinds = [(p >= sq), (p % 4 == sq), (p <= sq) & (p < 4), (p == sq)]
    smask = np.stack([np.tile(m.astype(np.float32), (1, 8)) for m in kinds], axis=1)
    return rope, amask, np.ascontiguousarray(smask)


def make_in_map(inp, core, n_ptiles=32, shared=None):
    b = core // 2
    T = n_ptiles * 128
    NT = n_ptiles + 1
    m = {}
    x = np.zeros((NT * 128, D), np.float32)
    x[:T] = inp["x_prompt"][b, :T]
    sl = slice(core * NSAMP, (core + 1) * NSAMP)
    x[T:T + 64] = np.asarray(inp["x_sample"][sl]).transpose(1, 0, 2).reshape(64, D)
    m["x_in"] = x
    if shared is None:
        shared = {}
        for k in WEIGHT_KEYS:
            shared[k] = np.ascontiguousarray(inp[k], dtype=np.float32)
        shared["norm_final"] = np.asarray(inp["norm_final"], np.float32).reshape(1, D)
        shared["ident"] = np.eye(128, dtype=np.float32)
        shared["rope_tab"], shared["amask"], shared["smask"] = _const_tables(n_ptiles)
    m.update(shared)
    m["state_lru_conv"] = np.ascontiguousarray(inp["state_lru_conv"][:, sl])
    m["state_lru_h"] = np.ascontiguousarray(inp["state_lru_h"][:, sl])
    m["state_cm_conv"] = np.ascontiguousarray(inp["state_cm_conv"][:, sl])
    m["cache_kv0"] = np.ascontiguousarray(inp["cache_kv_w128"][0, sl])
    m["cache_kv1"] = np.ascontiguousarray(inp["cache_kv_w512"][0, sl])
    c2 = np.asarray(inp["cache_kv_w2048"][0, sl])
    c2 = c2.reshape(NSAMP, 128, 16, 2, 8, 128)[:, :, 0:4].reshape(NSAMP, 512, 2, 8, 128)
    m["cache_kv2"] = np.ascontiguousarray(c2)
    return m, shared


_NC_CACHE = {}


def kernel(**inputs):
    inp = {k: np.asarray(v) for k, v in inputs.items()}
    if "nc" not in _NC_CACHE:
        _NC_CACHE["nc"] = build(n_ptiles=32, phases=("lru", "conv", "attn", "sattn", "ffn"), depth=4)
    nc = _NC_CACHE["nc"]
    in_maps = []
    shared = None
    for core in range(8):
        m, shared = make_in_map(inp, core, 32, shared)
        in_maps.append(m)
    res = run_bass_kernel_spmd(nc, in_maps, core_ids=list(range(8)))
    R = res.results
    T = SEQ
    B = 4
    y_prompt = np.stack([R[2 * b]["y_out"][:T] for b in range(B)]).astype(np.float32)
    y_sample = np.concatenate([R[c]["y_out"][T:T + 64].reshape(4, NSAMP, D).transpose(1, 0, 2) for c in range(8)], axis=0)
    def pstack(name):
        return np.stack([R[2 * b][name] for b in range(B)], axis=1)

    def scat(name):
        return np.concatenate([R[c][name] for c in range(8)], axis=1)
    outs = [y_prompt, np.ascontiguousarray(y_sample),
            pstack("p_lru_conv"), pstack("p_lru_h"), pstack("p_cm_conv"),
            np.stack([R[2 * b]["p_kv0"][0] for b in range(B)])[None],
            np.stack([R[2 * b]["p_kv1"][0] for b in range(B)])[None],
            np.stack([R[2 * b]["p_kv2"][0] for b in range(B)])[None],
            scat("s_lru_conv"), scat("s_lru_h"), scat("s_cm_conv"),
            np.concatenate([R[c]["s_kv0"].reshape(4, NSAMP, 2, 8, 128).transpose(1, 0, 2, 3, 4) for c in range(8)], axis=0)[None],
            np.concatenate([R[c]["s_kv1"].reshape(4, NSAMP, 2, 8, 128).transpose(1, 0, 2, 3, 4) for c in range(8)], axis=0)[None],
            np.concatenate([R[c]["s_kv2"].reshape(4, NSAMP, 2, 8, 128).transpose(1, 0, 2, 3, 4) for c in range(8)], axis=0)[None]]
    return tuple(np.ascontiguousarray(o, dtype=np.float32) for o in outs)
```
